# Optimizing a Trainium2 kernel written in Bass

```python
import math
import jax, jax.numpy as jnp
from jax import lax
import numpy as np

D_MODEL = 1024
BATCH = 16
SEQ = 2048
DEPTH = 4

CHUNK = 64
Q_BLOCK = 128
SB_HEADS = 8
SB_HEAD_DIM = 64
SB_WIDTH = SB_HEADS * SB_HEAD_DIM
DA_HEADS = 4
DA_HEAD_DIM = 64
DA_V_DIM = 2 * DA_HEAD_DIM
DA_WIDTH = DA_HEADS * DA_V_DIM
MIX_WIDTH = SB_WIDTH + DA_WIDTH
IN_PROJ = 3 * SB_WIDTH + 2 * (DA_HEADS * 2 * DA_HEAD_DIM) + DA_WIDTH
D_FF = 2816
CONV_WIDTH = 3
N_MOD = 6
EPS = 1e-6
NEG_INF = -1e30

kernel_name = "chunk_causal_hybrid_sb_diff_trunk"


def rms_norm(x, g):
    xf = x.astype(jnp.float32)
    y = xf * lax.rsqrt(jnp.mean(xf * xf, axis=-1, keepdims=True) + EPS)
    return (y * g.astype(jnp.float32)).astype(x.dtype)


def alibi_slopes(n_heads):
    return jnp.array([2.0 ** (-8.0 * (h + 1) / n_heads) for h in range(n_heads)], dtype=jnp.float32)


def stick_breaking_attention(q, k, v):
    S, dh = q.shape[2], q.shape[3]
    scale = dh ** -0.5
    outs = []
    for i in range(S // Q_BLOCK):
        t0, end = i * Q_BLOCK, (i + 1) * Q_BLOCK
        qb, kb, vb = q[:, :, t0:end], k[:, :, :end], v[:, :, :end]
        z = jnp.einsum("bhqd,bhkd->bhqk", qb, kb).astype(jnp.float32) * scale
        t_pos = t0 + jnp.arange(Q_BLOCK)
        s_pos = jnp.arange(end)
        mask = s_pos[None, :] < t_pos[:, None]
        log_beta = jax.nn.log_sigmoid(z)
        log_rem = jnp.where(mask, jax.nn.log_sigmoid(-z), 0.0)
        suffix = lax.cumsum(log_rem, axis=3, reverse=True) - log_rem
        w = jnp.where(mask, jnp.exp(log_beta + suffix), 0.0)
        outs.append(jnp.einsum("bhqk,bhkd->bhqd", w.astype(vb.dtype), vb))
    return jnp.concatenate(outs, axis=2)


def differential_attention(q, k, v, lam, slopes):
    S, dh = q.shape[3], q.shape[4]
    scale = dh ** -0.5
    outs = []
    for i in range(S // Q_BLOCK):
        t0, end = i * Q_BLOCK, (i + 1) * Q_BLOCK
        qb, kb, vb = q[:, :, :, t0:end], k[:, :, :, :end], v[:, :, :end]
        t_pos = t0 + jnp.arange(Q_BLOCK)
        s_pos = jnp.arange(end)
        dist = jnp.abs(t_pos[:, None] - s_pos[None, :]).astype(jnp.float32)
        bias = -slopes[:, None, None] * dist
        mask = (s_pos[None, :] // CHUNK) <= (t_pos[:, None] // CHUNK)
        sc = jnp.einsum("bhmqd,bhmkd->bhmqk", qb, kb).astype(jnp.float32) * scale
        sc = jnp.where(mask, sc + bias[None, :, None], NEG_INF)
        p = jax.nn.softmax(sc, axis=-1)
        attn = p[:, :, 0] - lam * p[:, :, 1]
        outs.append(jnp.einsum("bhqk,bhkd->bhqd", attn.astype(vb.dtype), vb))
    return jnp.concatenate(outs, axis=2)


def causal_depthwise_conv(u, w, b):
    C = u.shape[-1]
    y = lax.conv_general_dilated(u, w[:, None, :], window_strides=(1,),
                                 padding=[(CONV_WIDTH - 1, 0)],
                                 dimension_numbers=("NWC", "WIO", "NWC"),
                                 feature_group_count=C)
    return y + b


def setup_inputs(seed: int = 0) -> dict:
    key = jax.random.key(seed)
    ks = jax.random.split(key, 20)
    f32 = jnp.float32
    nrm = lambda k, shape, s: jax.random.normal(k, shape, f32) * s
    return {
        "x": nrm(ks[0], (BATCH, SEQ, D_MODEL), 1.0),
        "c": nrm(ks[1], (BATCH, D_MODEL), 1.0),
        "ada_w": nrm(ks[2], (DEPTH, D_MODEL, N_MOD * D_MODEL), D_MODEL ** -0.5),
        "ada_b": nrm(ks[3], (DEPTH, N_MOD * D_MODEL), 0.01),
        "attn_pre_g": 1.0 + nrm(ks[4], (DEPTH, D_MODEL), 0.02),
        "attn_post_g": 1.0 + nrm(ks[5], (DEPTH, D_MODEL), 0.02),
        "w_in": nrm(ks[6], (DEPTH, D_MODEL, IN_PROJ), D_MODEL ** -0.5),
        "w_out": nrm(ks[7], (DEPTH, MIX_WIDTH, D_MODEL), MIX_WIDTH ** -0.5),
        "lambda_q1": nrm(ks[8], (DEPTH, DA_HEAD_DIM), 0.1),
        "lambda_k1": nrm(ks[9], (DEPTH, DA_HEAD_DIM), 0.1),
        "lambda_q2": nrm(ks[10], (DEPTH, DA_HEAD_DIM), 0.1),
        "lambda_k2": nrm(ks[11], (DEPTH, DA_HEAD_DIM), 0.1),
        "da_subln_g": 1.0 + nrm(ks[12], (DEPTH, DA_V_DIM), 0.02),
        "ffn_pre_g": 1.0 + nrm(ks[13], (DEPTH, D_MODEL), 0.02),
        "ffn_post_g": 1.0 + nrm(ks[14], (DEPTH, D_MODEL), 0.02),
        "w_up": nrm(ks[15], (DEPTH, D_MODEL, 2 * D_FF), D_MODEL ** -0.5),
        "conv_w": nrm(ks[16], (DEPTH, CONV_WIDTH, 2 * D_FF), CONV_WIDTH ** -0.5),
        "conv_b": nrm(ks[17], (DEPTH, 2 * D_FF), 0.01),
        "w_down": nrm(ks[18], (DEPTH, D_FF, D_MODEL), D_FF ** -0.5),
    }


def reference(x, c, ada_w, ada_b, attn_pre_g, attn_post_g, w_in, w_out,
              lambda_q1, lambda_k1, lambda_q2, lambda_k2, da_subln_g,
              ffn_pre_g, ffn_post_g, w_up, conv_w, conv_b, w_down):
    B, S, D = x.shape
    slopes = alibi_slopes(DA_HEADS)
    c_act = jax.nn.silu(c)
    n_daqk = DA_HEADS * 2 * DA_HEAD_DIM
    for l in range(DEPTH):
        mod = c_act @ ada_w[l] + ada_b[l]
        sh_a, sc_a, g_a, sh_f, sc_f, g_f = [m[:, None, :] for m in jnp.split(mod, N_MOD, axis=-1)]

        h = rms_norm(x, attn_pre_g[l]) * (1.0 + sc_a) + sh_a
        proj = h @ w_in[l]
        sb_q, sb_k, sb_v, da_q, da_k, da_v = jnp.split(
            proj, np.cumsum([SB_WIDTH, SB_WIDTH, SB_WIDTH, n_daqk, n_daqk]).tolist(), axis=-1)
        to_heads = lambda t, hd: t.reshape(B, S, -1, hd).transpose(0, 2, 1, 3)
        sb_out = stick_breaking_attention(to_heads(sb_q, SB_HEAD_DIM), to_heads(sb_k, SB_HEAD_DIM),
                                          to_heads(sb_v, SB_HEAD_DIM))
        sb_out = sb_out.transpose(0, 2, 1, 3).reshape(B, S, SB_WIDTH)

        lambda_init = 0.8 - 0.6 * math.exp(-0.3 * l)
        lam = (jnp.exp(jnp.sum(lambda_q1[l].astype(jnp.float32) * lambda_k1[l].astype(jnp.float32)))
               - jnp.exp(jnp.sum(lambda_q2[l].astype(jnp.float32) * lambda_k2[l].astype(jnp.float32)))
               + lambda_init)
        dq = da_q.reshape(B, S, DA_HEADS, 2, DA_HEAD_DIM).transpose(0, 2, 3, 1, 4)
        dk = da_k.reshape(B, S, DA_HEADS, 2, DA_HEAD_DIM).transpose(0, 2, 3, 1, 4)
        dv = to_heads(da_v, DA_V_DIM)
        da_out = differential_attention(dq, dk, dv, lam, slopes)
        da_out = rms_norm(da_out, da_subln_g[l]) * (1.0 - lambda_init)
        da_out = da_out.transpose(0, 2, 1, 3).reshape(B, S, DA_WIDTH)

        mixed = jnp.concatenate([sb_out, da_out], axis=-1) @ w_out[l]
        x = x + g_a * rms_norm(mixed, attn_post_g[l])

        h = rms_norm(x, ffn_pre_g[l]) * (1.0 + sc_f) + sh_f
        u = causal_depthwise_conv(h @ w_up[l], conv_w[l], conv_b[l])
        gate, val = jnp.split(u, 2, axis=-1)
        f = (jax.nn.silu(gate) * val) @ w_down[l]
        x = x + g_f * rms_norm(f, ffn_post_g[l])
    return x
```

```python
import math
from contextlib import ExitStack
import numpy as np
import concourse.bass as bass
import concourse.mybir as mybir
from concourse.bass_utils import run_bass_kernel_spmd

F32 = mybir.dt.float32
BF16 = mybir.dt.bfloat16
AF = mybir.ActivationFunctionType
ALU = mybir.AluOpType

D = 1024
SEQ = 2048
NT = 16
DEPTH = 4
DFF = 2816
NFC = 22
EPS = 1e-6
NCORES = 8
SLOPES = [2.0 ** (-8.0 * (h + 1) / 4) for h in range(4)]

SAME_ENGINE_SYNC = True
GEN = 30000


class Buf:
    __slots__ = ("name", "last_w", "readers")

    def __init__(self, name):
        self.name = name
        self.last_w = None
        self.readers = []


class Op:
    __slots__ = ("eng", "fn", "deps", "needs_inc", "seq", "is_dma", "chan", "cum")

    def __init__(self, eng, fn, is_dma=False, chan=None):
        self.eng = eng
        self.fn = fn
        self.deps = []
        self.needs_inc = False
        self.seq = None
        self.is_dma = is_dma
        self.chan = chan
        self.cum = None


class Sched:
    ENGS = ("pe", "act", "dve", "pool", "sp")

    def __init__(self, nc):
        self.nc = nc
        self.ops = {e: [] for e in self.ENGS}
        self.chan_count = {}

    def buf(self, name="b"):
        return Buf(name)

    def bufs(self, n, name="b"):
        return [Buf(f"{name}{i}") for i in range(n)]

    def alias(self, news, olds):
        for nb in news:
            for ob in olds:
                if ob.last_w is not None:
                    nb.readers.append(ob.last_w)
                nb.readers.extend(ob.readers)

    def _record(self, op, reads, writes):
        deps = []
        for b in reads:
            if b.last_w is not None:
                deps.append((b.last_w, True))
        for b in writes:
            if b.last_w is not None:
                deps.append((b.last_w, True))
            deps.extend((r, False) for r in b.readers)
        seen = set()
        for d, is_w in deps:
            if d is op or id(d) in seen:
                continue
            if d.eng == op.eng and not d.is_dma and not op.is_dma:
                if op.eng == "pe" or not SAME_ENGINE_SYNC or not is_w:
                    continue
            seen.add(id(d))
            op.deps.append(d)
            d.needs_inc = True
        for b in reads:
            b.readers.append(op)
        for b in writes:
            b.last_w = op
            b.readers = []
        self.ops[op.eng].append(op)
        return op

    def op(self, eng, fn, reads=(), writes=()):
        return self._record(Op(eng, fn), reads, writes)

    def dma(self, eng, fn, chan, reads=(), writes=()):
        op = Op(eng, fn, is_dma=True, chan=chan)
        self.chan_count[chan] = self.chan_count.get(chan, 0) + 1
        op.cum = self.chan_count[chan]
        return self._record(op, reads, writes)

    def emit(self, final_wait_chans=()):
        nc = self.nc
        nsem = {}
        for e in self.ENGS:
            c = 0
            for op in self.ops[e]:
                if op.is_dma:
                    continue
                if op.needs_inc:
                    c += 1
                    op.seq = c
            nsem[e] = (c + GEN - 1) // GEN if c else 0
        with ExitStack() as es:
            esems = {e: [es.enter_context(nc.semaphore(f"s_{e}{i}")) for i in range(nsem[e])] for e in self.ENGS}
            csems = {c: es.enter_context(nc.semaphore(f"c_{c}")) for c in self.chan_count}
            block = es.enter_context(nc.Block())

            def run(ename, eng):
                known = {}
                for op in self.ops[ename]:
                    need = {}
                    for d in op.deps:
                        if d.is_dma:
                            key = ("c", d.chan)
                            val = 16 * d.cum
                            sem = csems[d.chan]
                        else:
                            g = (d.seq - 1) // GEN
                            key = (d.eng, g)
                            val = d.seq - g * GEN
                            sem = esems[d.eng][g]
                        if known.get(key, 0) >= val:
                            continue
                        if key not in need or need[key][1] < val:
                            need[key] = (sem, val)
                    for key, (sem, val) in need.items():
                        eng.wait_ge(sem, val)
                        known[key] = val
                    ins = op.fn(eng)
                    if op.is_dma:
                        ins.then_inc(csems[op.chan], 16)
                    elif op.needs_inc:
                        g = (op.seq - 1) // GEN
                        ins.then_inc(esems[ename][g], 1)
                if ename == "sp":
                    for c in final_wait_chans:
                        eng.wait_ge(csems[c], 16 * self.chan_count[c])

            block.tensor(lambda eng: run("pe", eng))
            block.scalar(lambda eng: run("act", eng))
            block.vector(lambda eng: run("dve", eng))
            block.gpsimd(lambda eng: run("pool", eng))
            block.sync(lambda eng: run("sp", eng))


def host_consts():
    j = np.arange(128)[:, None]
    s = np.arange(128)[None, :]
    c = {}
    c["ident"] = np.eye(128, dtype=np.float32)
    c["ntri"] = np.where(j >= s, -1.0, 0.0).astype(np.float32)
    sel = np.zeros((128, 128), np.float32)
    sel[0, :] = -1.0
    sel[32, :] = -1.0
    c["sel"] = sel
    oc = np.zeros((128, 16, 16), np.float32)
    for jb in range(16):
        oc[:, jb, jb] = 1.0
    c["onescol"] = oc.reshape(128, 256)
    c["msk_sp"] = (j < s).astype(np.float32)
    c["negmask"] = np.where(j < s, 0.0, -30000.0).astype(np.float32)
    dc = np.zeros((128, 4, 128), np.float32)
    for h in range(4):
        p = j
        f = s
        val = np.where(f >= p, 0.0, 2.0 * SLOPES[h] * (f - p))
        allowed = (p // 64) <= (f // 64)
        dc[:, h, :] = np.where(allowed, val, -30000.0)
    c["dcorr"] = dc.reshape(128, 512)
    c["ones"] = np.ones((128, 128), np.float32)
    t = np.arange(SEQ)
    aug = np.zeros((3, 4, 2, SEQ), np.float32)
    for h in range(4):
        aug[0, h, 0] = 1.0
        aug[1, h, 0] = 1.0
        aug[2, h, 0] = -SLOPES[h] * 128.0 * (t // 128)
        aug[0, h, 1] = SLOPES[h] * (t % 128)
        aug[1, h, 1] = SLOPES[h] * 128.0 * (t // 128)
        aug[2, h, 1] = 1.0
    c["aug"] = aug.reshape(3, 8 * SEQ)
    c["ident32"] = np.eye(128, dtype=np.float32)
    return c


CONST_SHAPES = {"ident": [128, 128], "ntri": [128, 128], "sel": [128, 128], "onescol": [128, 256], "msk_sp": [128, 128],
                "negmask": [128, 128], "dcorr": [128, 512], "ones": [128, 128], "aug": [3, 8 * SEQ], "ident32": [128, 128]}


STAGES = None
STAGES1 = None


def on(st):
    return STAGES is None or st in STAGES


def build(depth=DEPTH, nseq=2):
    nc = bass.Bass("TRN2", target_bir_lowering=False)
    dr = {}

    def din(name, shape):
        dr[name] = nc.dram_tensor(name, shape, F32, kind="ExternalInput").ap()
        return dr[name]

    x_d = din("x", [2, SEQ, D])
    cT_d = din("cT", [D, 2])
    ada_w_d = din("ada_w", [DEPTH, D, 6 * D])
    ada_b_d = din("ada_b2", [DEPTH, 2, 6 * D])
    w_in_d = din("w_in", [DEPTH, D, 3072])
    w_out_d = din("w_out", [DEPTH, D, D])
    w_up_d = din("w_up", [DEPTH, D, 2 * DFF])
    w_down_d = din("w_down", [DEPTH, DFF, D])
    vecT_d = din("vecT", [128, 4 * DEPTH * 8])
    convT_d = din("convT", [128, DEPTH * 4 * 44])
    sublnT_d = din("sublnT", [128, DEPTH])
    lamrep_d = din("lamrep", [128, DEPTH * 4 * 64])
    cd = {k: din("c_" + k, v) for k, v in CONST_SHAPES.items()}
    y_d = nc.dram_tensor("y", [2, SEQ, D], F32, kind="ExternalOutput").ap()

    es = ExitStack()
    with es:
        def sb(name, shape, dt=F32):
            return es.enter_context(nc.sbuf_tensor("s_" + name, shape, dt))

        def ps(name, shape, dt=F32):
            return es.enter_context(nc.psum_tensor("p_" + name, shape, dt))

        S = Sched(nc)
        x_sb = sb("x_sb", [128, NT * D])
        hT = sb("hT", [128, 8 * SEQ], BF16)
        R = sb("R", [128, 41984], BF16)
        W32 = sb("W32", [128, 3072])
        gp = sb("gp", [128, D])
        ident = sb("ident", [128, 128], BF16)
        ntri = sb("ntri", [128, 128], BF16)
        sel = sb("sel", [128, 128], BF16)
        onescol = sb("onescol", [128, 256], BF16)
        msk_sp = sb("msk_sp", [128, 128], BF16)
        negmask = sb("negmask", [128, 128], BF16)
        dcorr = sb("dcorr", [128, 512], BF16)
        ones_bf = sb("ones_bf", [128, 128], BF16)
        ident32 = sb("ident32", [128, 128])
        ones32 = sb("ones32", [128, 128])
        csp = sb("csp", [128, 512], BF16)
        modT = sb("modT", [128, DEPTH * 48 * 2])
        vecT = sb("vecT", [128, 4 * DEPTH * 8])
        convT = sb("convT", [128, 4 * 44])
        sublnT = sb("sublnT", [128, DEPTH])
        lamv = sb("lamv", [128, 16])
        gsub = sb("gsub", [128, DEPTH])
        cact = sb("cact", [128, 16])
        small = sb("small", [128, 128])
        halo = sb("halo", [128, 44 * 2])
        diag = sb("diag", [128, 128])

        P = [ps(f"P{i}", [128, 512]) for i in range(6)]
        PT = [ps(f"PT{i}", [128, 1024], BF16) for i in range(2)]
        PB = S.bufs(6, "PB")
        PTB = S.bufs(2, "PTB")

        O_AOT = 0
        O_QK = 16384
        O_V = O_QK + 8192
        O_SP = O_V + 2048
        O_WIN = O_SP + 8192
        O_WT = O_WIN + 3072
        O_EB = O_WT + 1024
        O_JUNK = O_EB + 1024
        O_ACT = 0
        O_WDN = 11264
        O_WUP = O_WDN + 22528

        def Rv(off, n):
            return R[:, off:off + n]

        xB = S.bufs(NT, "x")
        hTB = S.bufs(4, "hT")
        aoB = S.bufs(4, "ao")
        qkB = S.buf("qk")
        vB = S.buf("v")
        spB = S.bufs(16, "sp")
        winB = S.buf("win")
        wtB = S.bufs(2, "wt")
        ebB = S.bufs(2, "eb")
        junkB = S.buf("junk")
        actB = S.bufs(NFC, "act")
        wdnB = S.buf("wdn")
        wupB = S.bufs(2, "wup")
        w32B = S.bufs(6, "w32")
        gpB = S.buf("gp")
        cspB = S.buf("csp")
        smallB = S.buf("small")
        modB = S.buf("mod")
        cB = S.buf("consts")
        haloB = S.buf("halo")
        diagB = S.buf("diag")
        woutB = S.buf("wout")
        xnA = S.buf("xnA")
        xnF = S.buf("xnF")
        ATT_BUFS = aoB + [qkB, vB] + spB + [winB] + wtB + ebB + [junkB, xnA]
        FFN_BUFS = actB + [wdnB] + wupB + [xnF]

        def w32(i, n=512):
            return W32[:, i * 512:i * 512 + n]

        def cload(tile, name, eng="pool"):
            S.dma(eng, lambda e: e.dma_start(out=tile[:], in_=cd[name]), "cst", writes=[cB])

        cload(ident, "ident"); cload(ntri, "ntri"); cload(sel, "sel"); cload(onescol, "onescol")
        cload(msk_sp, "msk_sp"); cload(negmask, "negmask"); cload(dcorr, "dcorr"); cload(ones_bf, "ones")
        cload(ident32, "ident32", "sp"); cload(ones32, "ones", "sp")
        S.dma("sp", lambda e: e.dma_start(out=vecT[:], in_=vecT_d), "cst", writes=[cB])
        S.dma("sp", lambda e: e.dma_start(out=sublnT[:], in_=sublnT_d), "cst", writes=[cB])
        lamrep = W32[:, 0:1024]
        S.dma("sp", lambda e: e.dma_start(out=lamrep, in_=lamrep_d), "cst", writes=[cB, w32B[0], w32B[1]])
        S.dma("sp", lambda e: e.dma_start(out=cact[:].rearrange("p (k b) -> p k b", b=2), in_=cT_d.rearrange("(k p) b -> p k b", p=128)), "cst", writes=[cB])
        S.op("dve", lambda e: e.memset(csp[:], 0.0), writes=[cspB])
        S.op("dve", lambda e: e.memset(halo[:], 0.0), writes=[haloB])
        S.op("act", lambda e: e.activation(out=cact[:], in_=cact[:], func=AF.Silu), reads=[cB], writes=[cB])
        S.op("dve", lambda e: e.tensor_tensor(out=lamrep[:].rearrange("p (l w d) -> p l w d", w=4, d=64)[:, :, 0:4:2, :],
                                                in0=lamrep[:].rearrange("p (l w d) -> p l w d", w=4, d=64)[:, :, 0:4:2, :],
                                                in1=lamrep[:].rearrange("p (l w d) -> p l w d", w=4, d=64)[:, :, 1:4:2, :], op=ALU.mult),
             reads=[cB], writes=[cB, w32B[0], w32B[1]])
        S.op("dve", lambda e: e.tensor_reduce(out=lamv[:, 0:8].rearrange("p (l w) -> p l w", w=2),
                                                in_=lamrep[:].rearrange("p (l w d) -> p l w d", w=4, d=64)[:, :, 0:4:2, :],
                                                axis=mybir.AxisListType.X, op=ALU.add), reads=[cB, w32B[0], w32B[1]], writes=[cB])
        S.op("act", lambda e: e.activation(out=lamv[:, 0:8], in_=lamv[:, 0:8], func=AF.Exp), reads=[cB], writes=[cB])
        for l in range(DEPTH):
            li = 0.8 - 0.6 * math.exp(-0.3 * l)
            S.op("dve", lambda e, l=l, li=li: e.tensor_tensor(out=lamv[:, 8 + l:9 + l], in0=lamv[:, 2 * l + 1:2 * l + 2], in1=lamv[:, 2 * l:2 * l + 1], op=ALU.subtract),
                 reads=[cB], writes=[cB])
            S.op("dve", lambda e, l=l, li=li: e.tensor_scalar(out=lamv[:, 8 + l:9 + l], in0=lamv[:, 8 + l:9 + l], scalar1=-li, scalar2=None, op0=ALU.add),
                 reads=[cB], writes=[cB])
            S.op("dve", lambda e, l=l, li=li: e.tensor_scalar(out=gsub[:, l:l + 1], in0=sublnT[:, l:l + 1], scalar1=(1.0 - li), scalar2=None, op0=ALU.mult),
                 reads=[cB], writes=[cB])

        MR = 8 * 1024
        for l in range(depth):
            S.dma("sp", lambda e, l=l: e.dma_start(out=x_sb[0:2, MR:MR + 6144], in_=ada_b_d[l]), "adab", writes=[xB[8]])
            for ch in range(12):
                slot = ch % 2
                dst = x_sb[:, slot * 4096:(slot + 1) * 4096].rearrange("p (k n) -> p k n", n=512)
                S.dma("sp", lambda e, l=l, ch=ch, dst=dst: e.dma_start(out=dst, in_=ada_w_d[l, :, ch * 512:(ch + 1) * 512].rearrange("(k p) n -> p k n", p=128)),
                      f"adaw{slot}", writes=[xB[slot]])
                for k in range(8):
                    S.op("pe", lambda e, k=k, slot=slot: e.matmul(P[2][0:2, :], lhsT=cact[:, 2 * k:2 * k + 2], rhs=x_sb[:, slot * 4096 + k * 512:slot * 4096 + (k + 1) * 512],
                                                                 start=(k == 0), stop=(k == 7)), reads=[xB[slot], cB], writes=[PB[2]])
                S.op("dve", lambda e, ch=ch: e.tensor_tensor(out=x_sb[0:2, MR + ch * 512:MR + (ch + 1) * 512], in0=P[2][0:2, :],
                                                             in1=x_sb[0:2, MR + ch * 512:MR + (ch + 1) * 512], op=ALU.add),
                     writes=[PB[2], xB[8]])
            for j in range(48):
                S.op("pe", lambda e, j=j: e.matmul(P[5][:, 2 * j:2 * j + 2], lhsT=x_sb[0:2, MR + j * 128:MR + (j + 1) * 128], rhs=ident32[0:2, 0:2], start=True, stop=True),
                     reads=[xB[8], cB], writes=[PB[5]])
            S.op("dve", lambda e, l=l: e.tensor_copy(out=modT[:, l * 96:(l + 1) * 96], in_=P[5][:, 0:96]), writes=[PB[5], modB])

        def modcol(l, j, b):
            o = l * 96 + j * 2 + b
            return modT[:, o:o + 1]

        def modvec(l, grp, b):
            o = l * 96 + grp * 16 + b
            return modT[:, o:o + 15:2]

        def vec(which, l):
            o = (which * DEPTH + l) * 8
            return vecT[:, o:o + 8]

        def prenorm(l, b, which, xn_off):
            sh_g, sc_g = (0, 1) if which == 0 else (3, 4)
            pre = vec(0 if which == 0 else 2, l)
            S.op("dve", lambda e: e.tensor_scalar(out=small[:, 32:40], in0=modvec(l, sc_g, b), scalar1=1.0, scalar2=None, op0=ALU.add), reads=[modB], writes=[smallB])
            S.op("dve", lambda e: e.tensor_tensor(out=small[:, 32:40], in0=small[:, 32:40], in1=pre, op=ALU.mult), reads=[cB], writes=[smallB])
            S.op("dve", lambda e: e.tensor_copy(out=small[:, 40:48], in_=modvec(l, sh_g, b)), reads=[modB], writes=[smallB])
            xnB = xnA if which == 0 else xnF
            for c4 in range(4):
                for t in range(4):
                    tt = 4 * c4 + t
                    xn = Rv(xn_off + t * 1024, 1024)
                    xs = x_sb[:, tt * 1024:(tt + 1) * 1024]
                    S.op("act", lambda e, xn=xn, xs=xs, tt=tt: e.activation(out=xn, in_=xs, func=AF.Square, accum_out=small[:, tt:tt + 1]), reads=[xB[tt]], writes=[xnB, smallB])
                    S.op("act", lambda e, tt=tt: e.activation(out=small[:, 16 + tt:17 + tt], in_=small[:, tt:tt + 1], func=AF.Ln, bias=EPS, scale=1.0 / D), writes=[smallB])
                    S.op("act", lambda e, tt=tt: e.activation(out=small[:, 16 + tt:17 + tt], in_=small[:, 16 + tt:17 + tt], func=AF.Exp, scale=-0.5), writes=[smallB])
                    S.op("dve", lambda e, xn=xn, xs=xs, tt=tt: e.tensor_scalar(out=xn, in0=xs, scalar1=small[:, 16 + tt:17 + tt], scalar2=None, op0=ALU.mult),
                         reads=[xB[tt], smallB], writes=[xnB])
                for k in range(8):
                    pt = k % 2
                    for t in range(4):
                        S.op("pe", lambda e, k=k, t=t, pt=pt: e.transpose(PT[pt][:, t * 128:(t + 1) * 128], Rv(xn_off + t * 1024 + k * 128, 128), ident[:]),
                             reads=[xnB, cB], writes=[PTB[pt]])
                    dst = hT[:, k * SEQ + c4 * 512:k * SEQ + (c4 + 1) * 512]
                    if k % 2 == 0:
                        S.op("act", lambda e, k=k, pt=pt, dst=dst: e.activation(out=dst, in_=PT[pt][:, 0:512], func=AF.Identity, scale=small[:, 32 + k:33 + k], bias=small[:, 40 + k:41 + k]),
                             reads=[smallB], writes=[PTB[pt], hTB[c4]])
                    else:
                        S.op("dve", lambda e, k=k, pt=pt, dst=dst: e.tensor_scalar(out=dst, in0=PT[pt][:, 0:512], scalar1=small[:, 32 + k:33 + k], scalar2=small[:, 40 + k:41 + k], op0=ALU.mult, op1=ALU.add),
                             reads=[smallB], writes=[PTB[pt], hTB[c4]])

        def make_gp(l, b, which):
            g_g = 2 if which == 0 else 5
            post = vec(1 if which == 0 else 3, l)
            S.op("dve", lambda e: e.tensor_tensor(out=small[:, 48:56], in0=modvec(l, g_g, b), in1=post, op=ALU.mult), reads=[modB, cB], writes=[smallB])
            for k in range(8):
                pb = 2 if k < 4 else 5
                S.op("dve", lambda e, k=k: e.tensor_scalar(out=diag[:], in0=ident32[:], scalar1=small[:, 48 + k:49 + k], scalar2=None, op0=ALU.mult), reads=[smallB, cB], writes=[diagB])
                S.op("pe", lambda e, k=k, pb=pb: e.matmul(P[pb][:, (k % 4) * 128:(k % 4 + 1) * 128], lhsT=ones32[:], rhs=diag[:], start=True, stop=True), reads=[diagB, cB], writes=[PB[pb]])
                if k % 4 == 3:
                    S.op("act", lambda e, k=k, pb=pb: e.activation(out=gp[:, (k // 4) * 512:(k // 4 + 1) * 512], in_=P[pb][:, :], func=AF.Copy), writes=[PB[pb], gpB])

        def postnorm_residual(tt, pa, pb_):
            junk = Rv(O_JUNK, 512)
            for hf, pi in enumerate((pa, pb_)):
                S.op("act", lambda e, hf=hf, pi=pi: e.activation(out=junk, in_=P[pi][:, :], func=AF.Square, accum_out=small[:, 56 + hf:57 + hf]), writes=[PB[pi], junkB, smallB])
            S.op("dve", lambda e: e.tensor_tensor(out=small[:, 58:59], in0=small[:, 56:57], in1=small[:, 57:58], op=ALU.add), writes=[smallB])
            S.op("act", lambda e: e.activation(out=small[:, 59:60], in_=small[:, 58:59], func=AF.Ln, bias=EPS, scale=1.0 / D), writes=[smallB])
            S.op("act", lambda e: e.activation(out=small[:, 59:60], in_=small[:, 59:60], func=AF.Exp, scale=-0.5), writes=[smallB])
            for hf, pi in enumerate((pa, pb_)):
                tmp = w32(4 + hf)
                S.op("dve", lambda e, hf=hf, pi=pi, tmp=tmp: e.scalar_tensor_tensor(out=tmp, in0=P[pi][:, :], scalar=small[:, 59:60], in1=gp[:, hf * 512:(hf + 1) * 512], op0=ALU.mult, op1=ALU.mult),
                     reads=[smallB, gpB], writes=[PB[pi], w32B[4 + hf]])
                xs = x_sb[:, tt * 1024 + hf * 512:tt * 1024 + (hf + 1) * 512]
                S.op("dve", lambda e, xs=xs, tmp=tmp: e.tensor_tensor(out=xs, in0=xs, in1=tmp, op=ALU.add), reads=[w32B[4 + hf]], writes=[xB[tt]])

        qk2d = lambda i: Rv(O_QK + i * 2048, 2048)
        v2d = Rv(O_V, 2048)
        win3 = Rv(O_WIN, 3072).rearrange("p (k n) -> p k n", n=384)

        def project_group(l, g):
            is_da = g >= 4
            if not is_da:
                cq, ck, cv = 128 * g, 512 + 128 * g, 1024 + 128 * g
            else:
                h = g - 4
                cq, ck, cv = 1536 + 128 * h, 2048 + 128 * h, 2560 + 128 * h
            for i, c0 in enumerate((cq, ck, cv)):
                S.dma("pool", lambda e, i=i, c0=c0: e.dma_start(out=win3[:, :, i * 128:(i + 1) * 128], in_=w_in_d[l, :, c0:c0 + 128].rearrange("(k p) n -> p k n", p=128)),
                      "win", writes=[winB])
            if is_da:
                h = g - 4
                for i in range(4):
                    qk = 0 if i < 2 else 1
                    S.dma("pool", lambda e, i=i, qk=qk, h=h: e.dma_start(out=qk2d(i)[64:67, :], in_=cd["aug"][:, (h * 2 + qk) * SEQ:(h * 2 + qk + 1) * SEQ]), "aug", writes=[qkB])
            for c4 in range(4):
                for i in range(2):
                    pz = (2 * c4 + i) % 2
                    for k in range(8):
                        S.op("pe", lambda e, i=i, k=k, pz=pz, c4=c4: e.matmul(P[pz][:, :], lhsT=win3[:, k, i * 128:(i + 1) * 128], rhs=hT[:, k * SEQ + c4 * 512:k * SEQ + (c4 + 1) * 512],
                                                                       start=(k == 0), stop=(k == 7)), reads=[winB, hTB[c4]], writes=[PB[pz]])
                    sc = 0.125 if i == 0 else 1.0
                    cols = slice(c4 * 512, (c4 + 1) * 512)
                    if not is_da:
                        dst = qk2d(i)[:, cols]
                        S.op("act", lambda e, dst=dst, pz=pz, sc=sc: e.activation(out=dst, in_=P[pz][:, :], func=AF.Copy, scale=sc), writes=[PB[pz], qkB])
                    else:
                        d0 = qk2d(2 * i)[0:64, cols]
                        d1 = qk2d(2 * i + 1)[0:64, cols]
                        S.op("act", lambda e, d0=d0, pz=pz, sc=sc: e.activation(out=d0, in_=P[pz][0:64, :], func=AF.Copy, scale=sc), writes=[PB[pz], qkB])
                        S.op("dve", lambda e, d1=d1, pz=pz, sc=sc: e.tensor_scalar(out=d1, in0=P[pz][64:128, :], scalar1=sc, scalar2=None, op0=ALU.mult), writes=[PB[pz], qkB])
                pv = 2 if c4 % 2 == 0 else 5
                for t in range(4):
                    tt = 4 * c4 + t
                    for k in range(8):
                        S.op("pe", lambda e, t=t, tt=tt, k=k, pv=pv: e.matmul(P[pv][:, t * 128:(t + 1) * 128], lhsT=hT[:, k * SEQ + tt * 128:k * SEQ + (tt + 1) * 128], rhs=win3[:, k, 256:384],
                                                                    start=(k == 0), stop=(k == 7)), reads=[winB, hTB[c4]], writes=[PB[pv]])
                S.op("dve", lambda e, c4=c4, pv=pv: e.tensor_copy(out=v2d[:, c4 * 512:(c4 + 1) * 512], in_=P[pv][:, :]), writes=[PB[pv], vB])

        def sb_attention(g):
            qT, kT = qk2d(0), qk2d(1)
            ZB = [0, 1, 3]
            cnt = [0]
            for hh in range(2):
                pr = slice(hh * 64, hh * 64 + 64)
                for c in range(4):
                    i0 = 4 * c
                    nkb = i0 + 4
                    po = 5 if (cnt[0] % 2 == 0) else 4
                    cnt[0] += 1
                    S.op("dve", lambda e: e.memset(csp[0:64, :], 0.0), writes=[cspB])
                    jof = lambda k, nkb=nkb: nkb - 1 - k
                    c0f = lambda k, i0=i0, nkb=nkb: max(0, (nkb - 1 - k) - i0) * 128

                    def S1(k, i0=i0, pr=pr, jof=jof, c0f=c0f):
                        j, c0, z = jof(k), c0f(k), ZB[k % 3]
                        S.op("pe", lambda e: e.matmul(P[z][:, c0:512], lhsT=kT[pr, j * 128:(j + 1) * 128], rhs=qT[pr, i0 * 128 + c0:i0 * 128 + 512], start=True, stop=True),
                             reads=[qkB], writes=[PB[z]])

                    def S2a(k, c0f=c0f):
                        c0, z = c0f(k), ZB[k % 3]
                        e32 = w32(k % 2)
                        S.op("act", lambda e: e.activation(out=e32[:, c0:512], in_=P[z][:, c0:512], func=AF.Exp), writes=[PB[z], w32B[k % 2]])

                    def S2b(k, i0=i0, jof=jof, c0f=c0f):
                        j, c0 = jof(k), c0f(k)
                        e32 = w32(k % 2)
                        spj = Rv(O_SP + (k % 3) * 512, 512)
                        S.op("act", lambda e: e.activation(out=spj[:, c0:512], in_=e32[:, c0:512], func=AF.Ln, bias=1.0, scale=1.0), reads=[w32B[k % 2]], writes=[spB[k % 3]])
                        if j >= i0:
                            S.op("dve", lambda e: e.tensor_tensor(out=spj[:, c0:c0 + 128], in0=spj[:, c0:c0 + 128], in1=msk_sp[:], op=ALU.mult), reads=[cB], writes=[spB[k % 3]])

                    def S3(k, i0=i0, pr=pr, jof=jof, c0f=c0f, nkb=nkb):
                        j, c0, z = jof(k), c0f(k), ZB[k % 3]
                        spj = Rv(O_SP + (k % 3) * 512, 512)
                        S.op("pe", lambda e: e.matmul(P[2][0:1, c0:512], lhsT=ones_bf[:, 0:1], rhs=spj[:, c0:512], start=(k == 0), stop=(k == nkb - 1), skip_group_check=True),
                             reads=[spB[k % 3], cB], writes=[PB[2]])
                        S.op("pe", lambda e: e.matmul(P[z][:, c0:512], lhsT=ntri[:], rhs=spj[:, c0:512], start=False, stop=False, skip_group_check=True), reads=[spB[k % 3], cB], writes=[PB[z]])
                        if k > 0:
                            S.op("pe", lambda e: e.matmul(P[z][:, c0:512], lhsT=sel[:, 0:128], rhs=csp[:, c0:512], start=False, stop=False, skip_group_check=True), reads=[cspB, cB], writes=[PB[z]])
                        if j >= i0:
                            S.op("pe", lambda e: e.matmul(P[z][:, c0:c0 + 128], lhsT=ident[:], rhs=negmask[:], start=False, stop=True, skip_group_check=True), reads=[cB], writes=[PB[z]])

                    def S4(k, c0f=c0f):
                        c0 = c0f(k)
                        S.op("dve", lambda e: e.tensor_copy(out=csp[0:1, c0:512], in_=P[2][0:1, c0:512]), writes=[PB[2], cspB])
                        S.op("dve", lambda e: e.scalar_tensor_tensor(out=csp[32:33, c0:512], in0=P[2][0:1, c0:512], scalar=1.0, in1=csp[0:1, c0:512], op0=ALU.mult, op1=ALU.subtract),
                             writes=[PB[2], cspB])

                    def S5(k, c0f=c0f):
                        c0, z = c0f(k), ZB[k % 3]
                        wT = Rv(O_WT + (k % 2) * 512, 512)
                        S.op("act", lambda e: e.activation(out=wT[:, c0:512], in_=P[z][:, c0:512], func=AF.Exp), writes=[PB[z], wtB[k % 2]])

                    def S6(k, jof=jof, c0f=c0f, nkb=nkb, po=po):
                        j, c0 = jof(k), c0f(k)
                        wT = Rv(O_WT + (k % 2) * 512, 512)
                        S.op("pe", lambda e: e.matmul(P[po][:, c0:512], lhsT=v2d[:, j * 128:(j + 1) * 128], rhs=wT[:, c0:512], start=(k == 0), stop=(k == nkb - 1), skip_group_check=True),
                             reads=[wtB[k % 2], vB], writes=[PB[po]])

                    for it in range(nkb + 3):
                        if it < nkb:
                            S1(it)
                            S2a(it)
                        if 0 <= it - 1 < nkb:
                            S2b(it - 1)
                        if 0 <= it - 2 < nkb:
                            S3(it - 2)
                            if it - 2 < nkb - 1:
                                S4(it - 2)
                            S5(it - 2)
                        if 0 <= it - 3 < nkb:
                            S6(it - 3)
                    dst = Rv(O_AOT + g * SEQ + i0 * 128, 512)[pr, :]
                    S.op("dve", lambda e, dst=dst, pr=pr, po=po: e.tensor_copy(out=dst, in_=P[po][pr, :]), writes=[PB[po], aoB[c]])

        def da_attention(l, h):
            qm = [qk2d(0), qk2d(1)]
            km = [qk2d(2), qk2d(3)]
            blocks = [(c, j) for c in range(8) for j in range(2 * c + 2)]

            def A(bi):
                c, j = blocks[bi]
                i0 = 2 * c
                c0 = max(0, j - i0) * 128
                pz = bi % 2
                isd = j >= i0
                for m in range(2):
                    S.op("pe", lambda e, m=m: e.matmul(P[pz][:, m * 256 + c0:(m + 1) * 256], lhsT=km[m][0:67, j * 128:(j + 1) * 128],
                                                       rhs=qm[m][0:67, i0 * 128 + c0:i0 * 128 + 256], start=True, stop=(not isd)), reads=[qkB], writes=[PB[pz]])
                    if isd:
                        S.op("pe", lambda e, m=m: e.matmul(P[pz][:, m * 256 + c0:m * 256 + c0 + 128], lhsT=ident[:], rhs=dcorr[:, h * 128:(h + 1) * 128], start=False, stop=True),
                             reads=[cB], writes=[PB[pz]])

            def rng(bi):
                c, j = blocks[bi]
                c0 = max(0, j - 2 * c) * 128
                return [(0, 512)] if c0 == 0 else [(128, 256), (384, 512)]

            def E(bi):
                pz = bi % 2
                eb = Rv(O_EB + pz * 512, 512)
                for (a, b_) in rng(bi):
                    S.op("act", lambda e, a=a, b_=b_: e.activation(out=eb[:, a:b_], in_=P[pz][:, a:b_], func=AF.Exp), writes=[PB[pz], ebB[pz]])

            def V(bi):
                c, j = blocks[bi]
                nkb = 2 * c + 2
                pz = bi % 2
                eb = Rv(O_EB + pz * 512, 512)
                po = 3 + (c % 2)
                pd = 5 if c % 2 == 0 else 2
                rs_ = rng(bi)
                for ri, (a, b_) in enumerate(rs_):
                    lastr = ri == len(rs_) - 1
                    S.op("pe", lambda e, a=a, b_=b_, lastr=lastr: e.matmul(P[po][:, a:b_], lhsT=v2d[:, j * 128:(j + 1) * 128], rhs=eb[:, a:b_], start=(j == 0), stop=(j == nkb - 1 and lastr)),
                         reads=[ebB[pz], vB], writes=[PB[po]])
                for ri, (a, b_) in enumerate(rs_):
                    lastr = ri == len(rs_) - 1
                    S.op("pe", lambda e, a=a, b_=b_, lastr=lastr: e.matmul(P[pd][:, a:b_], lhsT=ones_bf[:], rhs=eb[:, a:b_], start=(j == 0), stop=(j == nkb - 1 and lastr)),
                         reads=[ebB[pz], cB], writes=[PB[pd]])
                if j == nkb - 1:
                    F1(c, po, pd)
                    pend.append([3, c, po])

            pend = []

            def F1(c, po, pd):
                rec, o12, o, rs = w32(0), w32(1), w32(2), w32(3)
                S.op("dve", lambda e: e.reciprocal(out=rec, in_=P[pd][:, :]), writes=[PB[pd], w32B[0]])
                S.op("dve", lambda e: e.tensor_tensor(out=o12, in0=P[po][:, :], in1=rec, op=ALU.mult), reads=[w32B[0]], writes=[PB[po], w32B[1]])
                S.op("dve", lambda e: e.scalar_tensor_tensor(out=o[:, 0:256], in0=o12[:, 256:512], scalar=lamv[:, 8 + l:9 + l], in1=o12[:, 0:256], op0=ALU.mult, op1=ALU.add),
                     reads=[w32B[1], cB], writes=[w32B[2]])
                osq = Rv(O_JUNK, 256)
                S.op("dve", lambda e: e.tensor_tensor(out=osq, in0=o[:, 0:256], in1=o[:, 0:256], op=ALU.mult), reads=[w32B[2]], writes=[junkB])

            def F2(c, po):
                i0 = 2 * c
                rec, o12, o, rs = w32(0), w32(1), w32(2), w32(3)
                osq = Rv(O_JUNK, 256)
                S.op("pe", lambda e: e.matmul(P[po][:, 0:256], lhsT=ones_bf[:], rhs=osq, start=True, stop=True), reads=[junkB, cB], writes=[PB[po]])
                S.op("act", lambda e: e.activation(out=rs[:, 0:256], in_=P[po][:, 0:256], func=AF.Ln, bias=EPS, scale=1.0 / 128), writes=[PB[po], w32B[3]])
                S.op("act", lambda e: e.activation(out=rs[:, 0:256], in_=rs[:, 0:256], func=AF.Exp, scale=-0.5), writes=[w32B[3]])
                S.op("dve", lambda e: e.tensor_tensor(out=o[:, 0:256], in0=o[:, 0:256], in1=rs[:, 0:256], op=ALU.mult), reads=[w32B[3]], writes=[w32B[2]])
                dst = Rv(O_AOT + (4 + h) * SEQ + i0 * 128, 256)
                S.op("act", lambda e: e.activation(out=dst, in_=o[:, 0:256], func=AF.Copy, scale=gsub[:, l:l + 1]), reads=[w32B[2], cB], writes=[aoB[c // 2]])

            def tick():
                for p_ in list(pend):
                    p_[0] -= 1
                    if p_[0] <= 0:
                        pend.remove(p_)
                        F2(p_[1], p_[2])

            nb = len(blocks)
            for bi in range(nb):
                A(bi)
                E(bi)
                if bi > 0:
                    V(bi - 1)
                tick()
            V(nb - 1)
            while pend:
                tick()

        wout3 = hT[:, 0:8192].rearrange("p (k n) -> p k n", n=1024)

        def out_proj(l, b):
            S.dma("pool", lambda e: e.dma_start(out=wout3, in_=w_out_d[l].rearrange("(k p) n -> p k n", p=128)), "wout", writes=hTB)
            make_gp(l, b, 0)
            for tt in range(NT):
                pa, pb_ = (0, 1) if tt % 2 == 0 else (3, 4)
                for hf, pi in enumerate((pa, pb_)):
                    for k in range(8):
                        S.op("pe", lambda e, k=k, hf=hf, pi=pi, tt=tt: e.matmul(P[pi][:, :], lhsT=Rv(O_AOT + k * SEQ + tt * 128, 128), rhs=wout3[:, k, hf * 512:(hf + 1) * 512], start=(k == 0), stop=(k == 7)),
                             reads=[aoB[tt // 4]] + hTB, writes=[PB[pi]])
                postnorm_residual(tt, pa, pb_)

        wdn3 = Rv(O_WDN, 22528).rearrange("p (f n) -> p f n", n=1024)

        def ffn(l, b):
            S.alias(FFN_BUFS, ATT_BUFS)
            S.alias(w32B, w32B)
            prenorm(l, b, 1, O_ACT)
            for f4 in range(0, NFC, 2):
                S.dma("pool", lambda e, f4=f4: e.dma_start(out=wdn3[:, f4:f4 + 2, :], in_=w_down_d[l, f4 * 128:(f4 + 2) * 128, :].rearrange("(f p) n -> p f n", p=128)), "wdn", writes=[wdnB])
            make_gp(l, b, 1)
            S.op("dve", lambda e: e.memset(halo[:], 0.0), writes=[haloB])
            convB = S.buf("conv")
            S.dma("sp", lambda e: e.dma_start(out=convT[:], in_=convT_d[:, l * 176:(l + 1) * 176]), "conv", writes=[convB, cB])
            cw = lambda i, f: convT[:, i * 44 + f:i * 44 + f + 1]
            for qd in range(4):
                for cg in range(11):
                    slot = cg % 2
                    wup3 = Rv(O_WUP + slot * 4096, 4096).rearrange("p (k n) -> p k n", n=512)
                    S.dma("pool", lambda e, cg=cg, wup3=wup3: e.dma_start(out=wup3[:, :, 0:256], in_=w_up_d[l, :, cg * 256:(cg + 1) * 256].rearrange("(k p) n -> p k n", p=128)), f"wup{slot}", writes=[wupB[slot]])
                    S.dma("pool", lambda e, cg=cg, wup3=wup3: e.dma_start(out=wup3[:, :, 256:512], in_=w_up_d[l, :, DFF + cg * 256:DFF + (cg + 1) * 256].rearrange("(k p) n -> p k n", p=128)), f"wup{slot}", writes=[wupB[slot]])
                    for cc in range(2):
                        fc = 2 * cg + cc
                        u3 = W32[:, 0:1028].rearrange("p (g n) -> p g n", n=514)
                        halo3 = halo[:, fc * 4:(fc + 1) * 4].rearrange("p (g t) -> p g t", t=2)
                        for gv in range(2):
                            for k in range(8):
                                S.op("pe", lambda e, k=k, gv=gv, cc=cc, wup3=wup3, qd=qd: e.matmul(P[gv][:, :], lhsT=wup3[:, k, gv * 256 + cc * 128:gv * 256 + (cc + 1) * 128],
                                                                                        rhs=hT[:, k * SEQ + qd * 512:k * SEQ + (qd + 1) * 512], start=(k == 0), stop=(k == 7)),
                                     reads=[wupB[slot], hTB[qd]], writes=[PB[gv]])
                        S.op("dve", lambda e, u3=u3, halo3=halo3: e.tensor_copy(out=u3[:, :, 0:2], in_=halo3), reads=[haloB], writes=[w32B[0], w32B[1]])
                        ys = []
                        for gv in range(2):
                            u = W32[:, gv * 514:(gv + 1) * 514]
                            fidx = gv * NFC + fc
                            y = W32[:, 1028 + gv * 512:1028 + (gv + 1) * 512]
                            S.op("act", lambda e, u=u, gv=gv: e.activation(out=u[:, 2:514], in_=P[gv][:, :], func=AF.Copy), writes=[PB[gv], w32B[gv]])
                            S.op("act", lambda e, y=y, gv=gv, fidx=fidx: e.activation(out=y, in_=P[gv][:, :], func=AF.Identity, scale=cw(2, fidx), bias=cw(3, fidx)),
                                 reads=[cB], writes=[PB[gv], w32B[2 + gv]])
                            ys.append(y)
                        S.op("dve", lambda e, u3=u3, halo3=halo3: e.tensor_copy(out=halo3, in_=u3[:, :, 512:514]), reads=[w32B[0], w32B[1]], writes=[haloB])
                        for gv in range(2):
                            u = W32[:, gv * 514:(gv + 1) * 514]
                            fidx = gv * NFC + fc
                            y = ys[gv]
                            S.op("dve", lambda e, u=u, y=y, fidx=fidx: e.scalar_tensor_tensor(out=y, in0=u[:, 1:513], scalar=cw(1, fidx), in1=y, op0=ALU.mult, op1=ALU.add),
                                 reads=[w32B[gv], cB], writes=[w32B[2 + gv]])
                            S.op("dve", lambda e, u=u, y=y, fidx=fidx: e.scalar_tensor_tensor(out=y, in0=u[:, 0:512], scalar=cw(0, fidx), in1=y, op0=ALU.mult, op1=ALU.add),
                                 reads=[w32B[gv], cB], writes=[w32B[2 + gv]])
                        sg = W32[:, 2052:2052 + 512]
                        S.op("act", lambda e, sg=sg, y=ys[0]: e.activation(out=sg, in_=y, func=AF.Silu), reads=[w32B[2]], writes=[w32B[4]])
                        dst = Rv(O_ACT + fc * 512, 512)
                        S.op("dve", lambda e, sg=sg, y=ys[1], dst=dst: e.tensor_tensor(out=dst, in0=sg, in1=y, op=ALU.mult), reads=[w32B[4], w32B[3]], writes=[actB[fc]])
                for t in range(4):
                    tt = 4 * qd + t
                    pa, pb_ = (3, 4) if t % 2 == 0 else (2, 5)
                    for hf, pi in enumerate((pa, pb_)):
                        for f in range(NFC):
                            S.op("pe", lambda e, f=f, hf=hf, pi=pi, t=t: e.matmul(P[pi][:, :], lhsT=Rv(O_ACT + f * 512 + t * 128, 128), rhs=wdn3[:, f, hf * 512:(hf + 1) * 512], start=(f == 0), stop=(f == NFC - 1)),
                                 reads=[actB[f], wdnB], writes=[PB[pi]])
                    postnorm_residual_ffn(tt, pa, pb_)
            S.alias(ATT_BUFS, FFN_BUFS)
            S.alias(w32B, w32B)

        def postnorm_residual_ffn(tt, pa, pb_):
            junk = W32[:, 1028:1028 + 512]
            for hf, pi in enumerate((pa, pb_)):
                S.op("act", lambda e, hf=hf, pi=pi: e.activation(out=junk, in_=P[pi][:, :], func=AF.Square, accum_out=small[:, 56 + hf:57 + hf]), writes=[PB[pi], w32B[2], smallB])
            S.op("dve", lambda e: e.tensor_tensor(out=small[:, 58:59], in0=small[:, 56:57], in1=small[:, 57:58], op=ALU.add), writes=[smallB])
            S.op("act", lambda e: e.activation(out=small[:, 59:60], in_=small[:, 58:59], func=AF.Ln, bias=EPS, scale=1.0 / D), writes=[smallB])
            S.op("act", lambda e: e.activation(out=small[:, 59:60], in_=small[:, 59:60], func=AF.Exp, scale=-0.5), writes=[smallB])
            for hf, pi in enumerate((pa, pb_)):
                tmp = W32[:, 2052:2052 + 512]
                S.op("dve", lambda e, hf=hf, pi=pi, tmp=tmp: e.scalar_tensor_tensor(out=tmp, in0=P[pi][:, :], scalar=small[:, 59:60], in1=gp[:, hf * 512:(hf + 1) * 512], op0=ALU.mult, op1=ALU.mult),
                     reads=[smallB, gpB], writes=[PB[pi], w32B[4]])
                xs = x_sb[:, tt * 1024 + hf * 512:tt * 1024 + (hf + 1) * 512]
                S.op("dve", lambda e, xs=xs, tmp=tmp: e.tensor_tensor(out=xs, in0=xs, in1=tmp, op=ALU.add), reads=[w32B[4]], writes=[xB[tt]])

        for b in range(nseq):
            for q4 in range(4):
                S.dma("sp", lambda e, b=b, q4=q4: e.dma_start(out=x_sb[:, q4 * 4096:(q4 + 1) * 4096].rearrange("p (t d) -> p t d", d=D),
                                                             in_=x_d[b, q4 * 512:(q4 + 1) * 512, :].rearrange("(t p) d -> p t d", p=128)), f"xin{q4}", writes=xB[4 * q4:4 * q4 + 4])
            for l in range(depth):
                on2 = (lambda st: on(st)) if l == 0 else (lambda st: STAGES1 is None or st in STAGES1)
                if on2("pre"):
                    prenorm(l, b, 0, O_SP)
                for g in range(8):
                    if on2("proj"):
                        project_group(l, g)
                    if g < 4:
                        if on2("sb"):
                            sb_attention(g)
                    else:
                        if on2("da"):
                            da_attention(l, g - 4)
                if on2("out"):
                    out_proj(l, b)
                if on2("ffn"):
                    ffn(l, b)
            for q4 in range(4):
                S.dma("sp", lambda e, b=b, q4=q4: e.dma_start(out=y_d[b, q4 * 512:(q4 + 1) * 512, :].rearrange("(t p) d -> p t d", p=128),
                                                             in_=x_sb[:, q4 * 4096:(q4 + 1) * 4096].rearrange("p (t d) -> p t d", d=D)), f"yout{q4}", reads=xB[4 * q4:4 * q4 + 4])
        S.emit(final_wait_chans=[f"yout{q}" for q in range(4)])
    return nc


def make_in_maps(inputs, ncores=NCORES):
    f = lambda a: np.ascontiguousarray(np.asarray(a, dtype=np.float32))
    x = f(inputs["x"]); c = f(inputs["c"])
    consts = host_consts()
    vec = np.stack([f(inputs[k]) for k in ("attn_pre_g", "attn_post_g", "ffn_pre_g", "ffn_post_g")], 0)
    vecT = np.ascontiguousarray(vec.reshape(4, DEPTH, 8, 128).transpose(3, 0, 1, 2).reshape(128, -1))
    cw = f(inputs["conv_w"]); cb = f(inputs["conv_b"])
    conv = np.concatenate([cw, cb[:, None, :]], 1)
    convT = np.ascontiguousarray(conv.reshape(DEPTH, 4, 44, 128).transpose(3, 0, 1, 2).reshape(128, -1))
    sublnT = np.ascontiguousarray(f(inputs["da_subln_g"]).T)
    lam = np.stack([f(inputs[k]) for k in ("lambda_q1", "lambda_k1", "lambda_q2", "lambda_k2")], 1)
    lamrep = np.ascontiguousarray(np.broadcast_to(lam.reshape(1, -1), (128, DEPTH * 4 * 64)))
    ada_b2 = np.ascontiguousarray(np.broadcast_to(f(inputs["ada_b"])[:, None, :], (DEPTH, 2, 6 * D)))
    shared = {"ada_w": f(inputs["ada_w"]), "ada_b2": ada_b2, "w_in": f(inputs["w_in"]), "w_out": f(inputs["w_out"]),
              "w_up": f(inputs["w_up"]), "w_down": f(inputs["w_down"]), "vecT": vecT, "convT": convT, "sublnT": sublnT, "lamrep": lamrep}
    for k, v in consts.items():
        shared["c_" + k] = np.ascontiguousarray(v)
    maps = []
    for i in range(ncores):
        m = dict(shared)
        m["x"] = np.ascontiguousarray(x[2 * i:2 * i + 2])
        m["cT"] = np.ascontiguousarray(c[2 * i:2 * i + 2].T)
        maps.append(m)
    return maps


_NC = None


def kernel(**inputs):
    global _NC
    if _NC is None:
        _NC = build()
    maps = make_in_maps(inputs)
    res = run_bass_kernel_spmd(_NC, maps, core_ids=list(range(NCORES)))
    out = np.concatenate([np.asarray(r["y"], dtype=np.float32) for r in res.results], axis=0)
    return out
```

```python
import math
from contextlib import ExitStack
import numpy as np
import concourse.bass as bass
import concourse.mybir as mybir
from concourse.bass_utils import run_bass_kernel_spmd

F32 = mybir.dt.float32
BF16 = mybir.dt.bfloat16
AF = mybir.ActivationFunctionType
ALU = mybir.AluOpType

D = 1024
SEQ = 2048
NT = 16
DEPTH = 4
DFF = 2816
NFC = 22
EPS = 1e-6
NCORES = 8
SLOPES = [2.0 ** (-8.0 * (h + 1) / 4) for h in range(4)]

SAME_ENGINE_SYNC = True
GEN = 30000


class Buf:
    __slots__ = ("name", "last_w", "readers")

    def __init__(self, name):
        self.name = name
        self.last_w = None
        self.readers = []


class Op:
    __slots__ = ("eng", "fn", "deps", "needs_inc", "seq", "is_dma", "chan", "cum")

    def __init__(self, eng, fn, is_dma=False, chan=None):
        self.eng = eng
        self.fn = fn
        self.deps = []
        self.needs_inc = False
        self.seq = None
        self.is_dma = is_dma
        self.chan = chan
        self.cum = None


class Sched:
    ENGS = ("pe", "act", "dve", "pool", "sp")

    def __init__(self, nc):
        self.nc = nc
        self.ops = {e: [] for e in self.ENGS}
        self.chan_count = {}

    def buf(self, name="b"):
        return Buf(name)

    def bufs(self, n, name="b"):
        return [Buf(f"{name}{i}") for i in range(n)]

    def alias(self, news, olds):
        for nb in news:
            for ob in olds:
                if ob.last_w is not None:
                    nb.readers.append(ob.last_w)
                nb.readers.extend(ob.readers)

    def _record(self, op, reads, writes):
        deps = []
        for b in reads:
            if b.last_w is not None:
                deps.append((b.last_w, True))
        for b in writes:
            if b.last_w is not None:
                deps.append((b.last_w, True))
            deps.extend((r, False) for r in b.readers)
        seen = set()
        for d, is_w in deps:
            if d is op or id(d) in seen:
                continue
            if d.eng == op.eng and not d.is_dma and not op.is_dma:
                if op.eng == "pe" or not SAME_ENGINE_SYNC or not is_w:
                    continue
            seen.add(id(d))
            op.deps.append(d)
            d.needs_inc = True
        for b in reads:
            b.readers.append(op)
        for b in writes:
            b.last_w = op
            b.readers = []
        self.ops[op.eng].append(op)
        return op

    def op(self, eng, fn, reads=(), writes=()):
        return self._record(Op(eng, fn), reads, writes)

    def dma(self, eng, fn, chan, reads=(), writes=()):
        op = Op(eng, fn, is_dma=True, chan=chan)
        self.chan_count[chan] = self.chan_count.get(chan, 0) + 1
        op.cum = self.chan_count[chan]
        return self._record(op, reads, writes)

    def emit(self, final_wait_chans=()):
        nc = self.nc
        nsem = {}
        for e in self.ENGS:
            c = 0
            for op in self.ops[e]:
                if op.is_dma:
                    continue
                if op.needs_inc:
                    c += 1
                    op.seq = c
            nsem[e] = (c + GEN - 1) // GEN if c else 0
        with ExitStack() as es:
            esems = {e: [es.enter_context(nc.semaphore(f"s_{e}{i}")) for i in range(nsem[e])] for e in self.ENGS}
            csems = {c: es.enter_context(nc.semaphore(f"c_{c}")) for c in self.chan_count}
            block = es.enter_context(nc.Block())

            def run(ename, eng):
                known = {}
                for op in self.ops[ename]:
                    need = {}
                    for d in op.deps:
                        if d.is_dma:
                            key = ("c", d.chan)
                            val = 16 * d.cum
                            sem = csems[d.chan]
                        else:
                            g = (d.seq - 1) // GEN
                            key = (d.eng, g)
                            val = d.seq - g * GEN
                            sem = esems[d.eng][g]
                        if known.get(key, 0) >= val:
                            continue
                        if key not in need or need[key][1] < val:
                            need[key] = (sem, val)
                    for key, (sem, val) in need.items():
                        eng.wait_ge(sem, val)
                        known[key] = val
                    ins = op.fn(eng)
                    if op.is_dma:
                        ins.then_inc(csems[op.chan], 16)
                    elif op.needs_inc:
                        g = (op.seq - 1) // GEN
                        ins.then_inc(esems[ename][g], 1)
                if ename == "sp":
                    for c in final_wait_chans:
                        eng.wait_ge(csems[c], 16 * self.chan_count[c])

            block.tensor(lambda eng: run("pe", eng))
            block.scalar(lambda eng: run("act", eng))
            block.vector(lambda eng: run("dve", eng))
            block.gpsimd(lambda eng: run("pool", eng))
            block.sync(lambda eng: run("sp", eng))


def host_consts():
    j = np.arange(128)[:, None]
    s = np.arange(128)[None, :]
    c = {}
    c["ident"] = np.eye(128, dtype=np.float32)
    c["ntri"] = np.where(j >= s, -1.0, 0.0).astype(np.float32)
    sel = np.zeros((128, 128), np.float32)
    sel[0, :] = -1.0
    sel[32, :] = -1.0
    c["sel"] = sel
    oc = np.zeros((128, 16, 16), np.float32)
    for jb in range(16):
        oc[:, jb, jb] = 1.0
    c["onescol"] = oc.reshape(128, 256)
    c["msk_sp"] = (j < s).astype(np.float32)
    c["negmask"] = np.where(j < s, 0.0, -30000.0).astype(np.float32)
    dc = np.zeros((128, 4, 128), np.float32)
    for h in range(4):
        p = j
        f = s
        val = np.where(f >= p, 0.0, 2.0 * SLOPES[h] * (f - p))
        allowed = (p // 64) <= (f // 64)
        dc[:, h, :] = np.where(allowed, val, -30000.0)
    c["dcorr"] = dc.reshape(128, 512)
    c["ones"] = np.ones((128, 128), np.float32)
    t = np.arange(SEQ)
    aug = np.zeros((3, 4, 2, SEQ), np.float32)
    for h in range(4):
        aug[0, h, 0] = 1.0
        aug[1, h, 0] = 1.0
        aug[2, h, 0] = -SLOPES[h] * 128.0 * (t // 128)
        aug[0, h, 1] = SLOPES[h] * (t % 128)
        aug[1, h, 1] = SLOPES[h] * 128.0 * (t // 128)
        aug[2, h, 1] = 1.0
    c["aug"] = aug.reshape(3, 8 * SEQ)
    c["ident32"] = np.eye(128, dtype=np.float32)
    return c


CONST_SHAPES = {"ident": [128, 128], "ntri": [128, 128], "sel": [128, 128], "onescol": [128, 256], "msk_sp": [128, 128],
                "negmask": [128, 128], "dcorr": [128, 512], "ones": [128, 128], "aug": [3, 8 * SEQ], "ident32": [128, 128]}


STAGES = None
STAGES1 = None


def on(st):
    return STAGES is None or st in STAGES


def build(depth=DEPTH, nseq=2):
    nc = bass.Bass("TRN2", target_bir_lowering=False)
    dr = {}

    def din(name, shape):
        dr[name] = nc.dram_tensor(name, shape, F32, kind="ExternalInput").ap()
        return dr[name]

    x_d = din("x", [2, SEQ, D])
    cT_d = din("cT", [D, 2])
    ada_w_d = din("ada_w", [DEPTH, D, 6 * D])
    ada_b_d = din("ada_b2", [DEPTH, 2, 6 * D])
    w_in_d = din("w_in", [DEPTH, D, 3072])
    w_out_d = din("w_out", [DEPTH, D, D])
    w_up_d = din("w_up", [DEPTH, D, 2 * DFF])
    w_down_d = din("w_down", [DEPTH, DFF, D])
    vecT_d = din("vecT", [128, 4 * DEPTH * 8])
    convT_d = din("convT", [128, DEPTH * 4 * 44])
    sublnT_d = din("sublnT", [128, DEPTH])
    lamrep_d = din("lamrep", [128, DEPTH * 4 * 64])
    cd = {k: din("c_" + k, v) for k, v in CONST_SHAPES.items()}
    y_d = nc.dram_tensor("y", [2, SEQ, D], F32, kind="ExternalOutput").ap()

    es = ExitStack()
    with es:
        def sb(name, shape, dt=F32):
            return es.enter_context(nc.sbuf_tensor("s_" + name, shape, dt))

        def ps(name, shape, dt=F32):
            return es.enter_context(nc.psum_tensor("p_" + name, shape, dt))

        S = Sched(nc)
        x_sb = sb("x_sb", [128, NT * D])
        hT = sb("hT", [128, 8 * SEQ], BF16)
        R = sb("R", [128, 41984], BF16)
        W32 = sb("W32", [128, 3072])
        gp = sb("gp", [128, D])
        ident = sb("ident", [128, 128], BF16)
        ntri = sb("ntri", [128, 128], BF16)
        sel = sb("sel", [128, 128], BF16)
        onescol = sb("onescol", [128, 256], BF16)
        msk_sp = sb("msk_sp", [128, 128], BF16)
        negmask = sb("negmask", [128, 128], BF16)
        dcorr = sb("dcorr", [128, 512], BF16)
        ones_bf = sb("ones_bf", [128, 128], BF16)
        ident32 = sb("ident32", [128, 128])
        ones32 = sb("ones32", [128, 128])
        csp = sb("csp", [128, 512], BF16)
        modT = sb("modT", [128, DEPTH * 48 * 2])
        vecT = sb("vecT", [128, 4 * DEPTH * 8])
        convT = sb("convT", [128, 4 * 44])
        sublnT = sb("sublnT", [128, DEPTH])
        lamv = sb("lamv", [128, 16])
        gsub = sb("gsub", [128, DEPTH])
        cact = sb("cact", [128, 16])
        small = sb("small", [128, 128])
        halo = sb("halo", [128, 44 * 2])
        diag = sb("diag", [128, 128])

        P = [ps(f"P{i}", [128, 512]) for i in range(6)]
        PT = [ps(f"PT{i}", [128, 1024], BF16) for i in range(2)]
        PB = S.bufs(6, "PB")
        PTB = S.bufs(2, "PTB")

        O_AOT = 0
        O_QK = 16384
        O_V = O_QK + 8192
        O_SP = O_V + 2048
        O_WIN = O_SP + 8192
        O_WT = O_WIN + 3072
        O_EB = O_WT + 1024
        O_JUNK = O_EB + 1024
        O_ACT = 0
        O_WDN = 11264
        O_WUP = O_WDN + 22528

        def Rv(off, n):
            return R[:, off:off + n]

        xB = S.bufs(NT, "x")
        hTB = S.bufs(4, "hT")
        aoB = S.bufs(4, "ao")
        qkB = S.buf("qk")
        vB = S.buf("v")
        spB = S.bufs(16, "sp")
        winB = S.buf("win")
        wtB = S.bufs(2, "wt")
        ebB = S.bufs(2, "eb")
        junkB = S.buf("junk")
        actB = S.bufs(NFC, "act")
        wdnB = S.buf("wdn")
        wupB = S.bufs(2, "wup")
        w32B = S.bufs(6, "w32")
        gpB = S.buf("gp")
        cspB = S.buf("csp")
        smallB = S.buf("small")
        modB = S.buf("mod")
        cB = S.buf("consts")
        haloB = S.buf("halo")
        diagB = S.buf("diag")
        woutB = S.buf("wout")
        xnA = S.buf("xnA")
        xnF = S.buf("xnF")
        ATT_BUFS = aoB + [qkB, vB] + spB + [winB] + wtB + ebB + [junkB, xnA]
        FFN_BUFS = actB + [wdnB] + wupB + [xnF]

        def w32(i, n=512):
            return W32[:, i * 512:i * 512 + n]

        def cload(tile, name, eng="pool"):
            S.dma(eng, lambda e: e.dma_start(out=tile[:], in_=cd[name]), "cst", writes=[cB])

        cload(ident, "ident"); cload(ntri, "ntri"); cload(sel, "sel"); cload(onescol, "onescol")
        cload(msk_sp, "msk_sp"); cload(negmask, "negmask"); cload(dcorr, "dcorr"); cload(ones_bf, "ones")
        cload(ident32, "ident32", "sp"); cload(ones32, "ones", "sp")
        S.dma("sp", lambda e: e.dma_start(out=vecT[:], in_=vecT_d), "cst", writes=[cB])
        S.dma("sp", lambda e: e.dma_start(out=sublnT[:], in_=sublnT_d), "cst", writes=[cB])
        lamrep = W32[:, 0:1024]
        S.dma("sp", lambda e: e.dma_start(out=lamrep, in_=lamrep_d), "cst", writes=[cB, w32B[0], w32B[1]])
        S.dma("sp", lambda e: e.dma_start(out=cact[:].rearrange("p (k b) -> p k b", b=2), in_=cT_d.rearrange("(k p) b -> p k b", p=128)), "cst", writes=[cB])
        S.op("dve", lambda e: e.memset(csp[:], 0.0), writes=[cspB])
        S.op("dve", lambda e: e.memset(halo[:], 0.0), writes=[haloB])
        S.op("act", lambda e: e.activation(out=cact[:], in_=cact[:], func=AF.Silu), reads=[cB], writes=[cB])
        S.op("dve", lambda e: e.tensor_tensor(out=lamrep[:].rearrange("p (l w d) -> p l w d", w=4, d=64)[:, :, 0:4:2, :],
                                                in0=lamrep[:].rearrange("p (l w d) -> p l w d", w=4, d=64)[:, :, 0:4:2, :],
                                                in1=lamrep[:].rearrange("p (l w d) -> p l w d", w=4, d=64)[:, :, 1:4:2, :], op=ALU.mult),
             reads=[cB], writes=[cB, w32B[0], w32B[1]])
        S.op("dve", lambda e: e.tensor_reduce(out=lamv[:, 0:8].rearrange("p (l w) -> p l w", w=2),
                                                in_=lamrep[:].rearrange("p (l w d) -> p l w d", w=4, d=64)[:, :, 0:4:2, :],
                                                axis=mybir.AxisListType.X, op=ALU.add), reads=[cB, w32B[0], w32B[1]], writes=[cB])
        S.op("act", lambda e: e.activation(out=lamv[:, 0:8], in_=lamv[:, 0:8], func=AF.Exp), reads=[cB], writes=[cB])
        for l in range(DEPTH):
            li = 0.8 - 0.6 * math.exp(-0.3 * l)
            S.op("dve", lambda e, l=l, li=li: e.tensor_tensor(out=lamv[:, 8 + l:9 + l], in0=lamv[:, 2 * l + 1:2 * l + 2], in1=lamv[:, 2 * l:2 * l + 1], op=ALU.subtract),
                 reads=[cB], writes=[cB])
            S.op("dve", lambda e, l=l, li=li: e.tensor_scalar(out=lamv[:, 8 + l:9 + l], in0=lamv[:, 8 + l:9 + l], scalar1=-li, scalar2=None, op0=ALU.add),
                 reads=[cB], writes=[cB])
            S.op("dve", lambda e, l=l, li=li: e.tensor_scalar(out=gsub[:, l:l + 1], in0=sublnT[:, l:l + 1], scalar1=(1.0 - li), scalar2=None, op0=ALU.mult),
                 reads=[cB], writes=[cB])

        MR = 8 * 1024
        for l in range(depth):
            S.dma("sp", lambda e, l=l: e.dma_start(out=x_sb[0:2, MR:MR + 6144], in_=ada_b_d[l]), "adab", writes=[xB[8]])
            for ch in range(12):
                slot = ch % 2
                dst = x_sb[:, slot * 4096:(slot + 1) * 4096].rearrange("p (k n) -> p k n", n=512)
                S.dma("sp", lambda e, l=l, ch=ch, dst=dst: e.dma_start(out=dst, in_=ada_w_d[l, :, ch * 512:(ch + 1) * 512].rearrange("(k p) n -> p k n", p=128)),
                      f"adaw{slot}", writes=[xB[slot]])
                for k in range(8):
                    S.op("pe", lambda e, k=k, slot=slot: e.matmul(P[2][0:2, :], lhsT=cact[:, 2 * k:2 * k + 2], rhs=x_sb[:, slot * 4096 + k * 512:slot * 4096 + (k + 1) * 512],
                                                                 start=(k == 0), stop=(k == 7)), reads=[xB[slot], cB], writes=[PB[2]])
                S.op("dve", lambda e, ch=ch: e.tensor_tensor(out=x_sb[0:2, MR + ch * 512:MR + (ch + 1) * 512], in0=P[2][0:2, :],
                                                             in1=x_sb[0:2, MR + ch * 512:MR + (ch + 1) * 512], op=ALU.add),
                     writes=[PB[2], xB[8]])
            for j in range(48):
                S.op("pe", lambda e, j=j: e.matmul(P[5][:, 2 * j:2 * j + 2], lhsT=x_sb[0:2, MR + j * 128:MR + (j + 1) * 128], rhs=ident32[0:2, 0:2], start=True, stop=True),
                     reads=[xB[8], cB], writes=[PB[5]])
            S.op("dve", lambda e, l=l: e.tensor_copy(out=modT[:, l * 96:(l + 1) * 96], in_=P[5][:, 0:96]), writes=[PB[5], modB])

        def modcol(l, j, b):
            o = l * 96 + j * 2 + b
            return modT[:, o:o + 1]

        def modvec(l, grp, b):
            o = l * 96 + grp * 16 + b
            return modT[:, o:o + 15:2]

        def vec(which, l):
            o = (which * DEPTH + l) * 8
            return vecT[:, o:o + 8]

        def prenorm(l, b, which, xn_off):
            sh_g, sc_g = (0, 1) if which == 0 else (3, 4)
            pre = vec(0 if which == 0 else 2, l)
            S.op("dve", lambda e: e.tensor_scalar(out=small[:, 32:40], in0=modvec(l, sc_g, b), scalar1=1.0, scalar2=None, op0=ALU.add), reads=[modB], writes=[smallB])
            S.op("dve", lambda e: e.tensor_tensor(out=small[:, 32:40], in0=small[:, 32:40], in1=pre, op=ALU.mult), reads=[cB], writes=[smallB])
            S.op("dve", lambda e: e.tensor_copy(out=small[:, 40:48], in_=modvec(l, sh_g, b)), reads=[modB], writes=[smallB])
            xnB = xnA if which == 0 else xnF
            for c4 in range(4):
                for t in range(4):
                    tt = 4 * c4 + t
                    xn = Rv(xn_off + t * 1024, 1024)
                    xs = x_sb[:, tt * 1024:(tt + 1) * 1024]
                    S.op("act", lambda e, xn=xn, xs=xs, tt=tt: e.activation(out=xn, in_=xs, func=AF.Square, accum_out=small[:, tt:tt + 1]), reads=[xB[tt]], writes=[xnB, smallB])
                    S.op("act", lambda e, tt=tt: e.activation(out=small[:, 16 + tt:17 + tt], in_=small[:, tt:tt + 1], func=AF.Ln, bias=EPS, scale=1.0 / D), writes=[smallB])
                    S.op("act", lambda e, tt=tt: e.activation(out=small[:, 16 + tt:17 + tt], in_=small[:, 16 + tt:17 + tt], func=AF.Exp, scale=-0.5), writes=[smallB])
                    S.op("dve", lambda e, xn=xn, xs=xs, tt=tt: e.tensor_scalar(out=xn, in0=xs, scalar1=small[:, 16 + tt:17 + tt], scalar2=None, op0=ALU.mult),
                         reads=[xB[tt], smallB], writes=[xnB])
                for k in range(8):
                    pt = k % 2
                    for t in range(4):
                        S.op("pe", lambda e, k=k, t=t, pt=pt: e.transpose(PT[pt][:, t * 128:(t + 1) * 128], Rv(xn_off + t * 1024 + k * 128, 128), ident[:]),
                             reads=[xnB, cB], writes=[PTB[pt]])
                    dst = hT[:, k * SEQ + c4 * 512:k * SEQ + (c4 + 1) * 512]
                    if k % 2 == 0:
                        S.op("act", lambda e, k=k, pt=pt, dst=dst: e.activation(out=dst, in_=PT[pt][:, 0:512], func=AF.Identity, scale=small[:, 32 + k:33 + k], bias=small[:, 40 + k:41 + k]),
                             reads=[smallB], writes=[PTB[pt], hTB[c4]])
                    else:
                        S.op("dve", lambda e, k=k, pt=pt, dst=dst: e.tensor_scalar(out=dst, in0=PT[pt][:, 0:512], scalar1=small[:, 32 + k:33 + k], scalar2=small[:, 40 + k:41 + k], op0=ALU.mult, op1=ALU.add),
                             reads=[smallB], writes=[PTB[pt], hTB[c4]])

        def make_gp(l, b, which):
            g_g = 2 if which == 0 else 5
            post = vec(1 if which == 0 else 3, l)
            S.op("dve", lambda e: e.tensor_tensor(out=small[:, 48:56], in0=modvec(l, g_g, b), in1=post, op=ALU.mult), reads=[modB, cB], writes=[smallB])
            for k in range(8):
                pb = 2 if k < 4 else 5
                S.op("dve", lambda e, k=k: e.tensor_scalar(out=diag[:], in0=ident32[:], scalar1=small[:, 48 + k:49 + k], scalar2=None, op0=ALU.mult), reads=[smallB, cB], writes=[diagB])
                S.op("pe", lambda e, k=k, pb=pb: e.matmul(P[pb][:, (k % 4) * 128:(k % 4 + 1) * 128], lhsT=ones32[:], rhs=diag[:], start=True, stop=True), reads=[diagB, cB], writes=[PB[pb]])
                if k % 4 == 3:
                    S.op("act", lambda e, k=k, pb=pb: e.activation(out=gp[:, (k // 4) * 512:(k // 4 + 1) * 512], in_=P[pb][:, :], func=AF.Copy), writes=[PB[pb], gpB])

        def postnorm_residual(tt, pa, pb_):
            junk = Rv(O_JUNK, 512)
            for hf, pi in enumerate((pa, pb_)):
                S.op("act", lambda e, hf=hf, pi=pi: e.activation(out=junk, in_=P[pi][:, :], func=AF.Square, accum_out=small[:, 56 + hf:57 + hf]), writes=[PB[pi], junkB, smallB])
            S.op("dve", lambda e: e.tensor_tensor(out=small[:, 58:59], in0=small[:, 56:57], in1=small[:, 57:58], op=ALU.add), writes=[smallB])
            S.op("act", lambda e: e.activation(out=small[:, 59:60], in_=small[:, 58:59], func=AF.Ln, bias=EPS, scale=1.0 / D), writes=[smallB])
            S.op("act", lambda e: e.activation(out=small[:, 59:60], in_=small[:, 59:60], func=AF.Exp, scale=-0.5), writes=[smallB])
            for hf, pi in enumerate((pa, pb_)):
                tmp = w32(4 + hf)
                S.op("dve", lambda e, hf=hf, pi=pi, tmp=tmp: e.scalar_tensor_tensor(out=tmp, in0=P[pi][:, :], scalar=small[:, 59:60], in1=gp[:, hf * 512:(hf + 1) * 512], op0=ALU.mult, op1=ALU.mult),
                     reads=[smallB, gpB], writes=[PB[pi], w32B[4 + hf]])
                xs = x_sb[:, tt * 1024 + hf * 512:tt * 1024 + (hf + 1) * 512]
                S.op("dve", lambda e, xs=xs, tmp=tmp: e.tensor_tensor(out=xs, in0=xs, in1=tmp, op=ALU.add), reads=[w32B[4 + hf]], writes=[xB[tt]])

        qk2d = lambda i: Rv(O_QK + i * 2048, 2048)
        v2d = Rv(O_V, 2048)
        win3 = Rv(O_WIN, 3072).rearrange("p (k n) -> p k n", n=384)

        def project_group(l, g):
            is_da = g >= 4
            if not is_da:
                cq, ck, cv = 128 * g, 512 + 128 * g, 1024 + 128 * g
            else:
                h = g - 4
                cq, ck, cv = 1536 + 128 * h, 2048 + 128 * h, 2560 + 128 * h
            for i, c0 in enumerate((cq, ck, cv)):
                S.dma("pool", lambda e, i=i, c0=c0: e.dma_start(out=win3[:, :, i * 128:(i + 1) * 128], in_=w_in_d[l, :, c0:c0 + 128].rearrange("(k p) n -> p k n", p=128)),
                      "win", writes=[winB])
            if is_da:
                h = g - 4
                for i in range(4):
                    qk = 0 if i < 2 else 1
                    S.dma("pool", lambda e, i=i, qk=qk, h=h: e.dma_start(out=qk2d(i)[64:67, :], in_=cd["aug"][:, (h * 2 + qk) * SEQ:(h * 2 + qk + 1) * SEQ]), "aug", writes=[qkB])
            for c4 in range(4):
                for i in range(2):
                    pz = (2 * c4 + i) % 2
                    for k in range(8):
                        S.op("pe", lambda e, i=i, k=k, pz=pz, c4=c4: e.matmul(P[pz][:, :], lhsT=win3[:, k, i * 128:(i + 1) * 128], rhs=hT[:, k * SEQ + c4 * 512:k * SEQ + (c4 + 1) * 512],
                                                                       start=(k == 0), stop=(k == 7)), reads=[winB, hTB[c4]], writes=[PB[pz]])
                    sc = 0.125 if i == 0 else 1.0
                    cols = slice(c4 * 512, (c4 + 1) * 512)
                    if not is_da:
                        dst = qk2d(i)[:, cols]
                        S.op("act", lambda e, dst=dst, pz=pz, sc=sc: e.activation(out=dst, in_=P[pz][:, :], func=AF.Copy, scale=sc), writes=[PB[pz], qkB])
                    else:
                        d0 = qk2d(2 * i)[0:64, cols]
                        d1 = qk2d(2 * i + 1)[0:64, cols]
                        S.op("act", lambda e, d0=d0, pz=pz, sc=sc: e.activation(out=d0, in_=P[pz][0:64, :], func=AF.Copy, scale=sc), writes=[PB[pz], qkB])
                        S.op("dve", lambda e, d1=d1, pz=pz, sc=sc: e.tensor_scalar(out=d1, in0=P[pz][64:128, :], scalar1=sc, scalar2=None, op0=ALU.mult), writes=[PB[pz], qkB])
                pv = 2 if c4 % 2 == 0 else 5
                for t in range(4):
                    tt = 4 * c4 + t
                    for k in range(8):
                        S.op("pe", lambda e, t=t, tt=tt, k=k, pv=pv: e.matmul(P[pv][:, t * 128:(t + 1) * 128], lhsT=hT[:, k * SEQ + tt * 128:k * SEQ + (tt + 1) * 128], rhs=win3[:, k, 256:384],
                                                                    start=(k == 0), stop=(k == 7)), reads=[winB, hTB[c4]], writes=[PB[pv]])
                S.op("dve", lambda e, c4=c4, pv=pv: e.tensor_copy(out=v2d[:, c4 * 512:(c4 + 1) * 512], in_=P[pv][:, :]), writes=[PB[pv], vB])

        def sb_attention(g):
            qT, kT = qk2d(0), qk2d(1)
            ZB = [0, 1, 3]
            cnt = [0]
            for hh in range(2):
                pr = slice(hh * 64, hh * 64 + 64)
                for c in range(4):
                    i0 = 4 * c
                    nkb = i0 + 4
                    po = 5 if (cnt[0] % 2 == 0) else 4
                    cnt[0] += 1
                    S.op("dve", lambda e: e.memset(csp[0:64, :], 0.0), writes=[cspB])
                    jof = lambda k, nkb=nkb: nkb - 1 - k
                    c0f = lambda k, i0=i0, nkb=nkb: max(0, (nkb - 1 - k) - i0) * 128

                    def S1(k, i0=i0, pr=pr, jof=jof, c0f=c0f):
                        j, c0, z = jof(k), c0f(k), ZB[k % 3]
                        S.op("pe", lambda e: e.matmul(P[z][:, c0:512], lhsT=kT[pr, j * 128:(j + 1) * 128], rhs=qT[pr, i0 * 128 + c0:i0 * 128 + 512], start=True, stop=True),
                             reads=[qkB], writes=[PB[z]])

                    def S2a(k, c0f=c0f):
                        c0, z = c0f(k), ZB[k % 3]
                        e32 = w32(k % 2)
                        S.op("act", lambda e: e.activation(out=e32[:, c0:512], in_=P[z][:, c0:512], func=AF.Exp), writes=[PB[z], w32B[k % 2]])

                    def S2b(k, i0=i0, jof=jof, c0f=c0f):
                        j, c0 = jof(k), c0f(k)
                        e32 = w32(k % 2)
                        spj = Rv(O_SP + (k % 3) * 512, 512)
                        S.op("act", lambda e: e.activation(out=spj[:, c0:512], in_=e32[:, c0:512], func=AF.Ln, bias=1.0, scale=1.0), reads=[w32B[k % 2]], writes=[spB[k % 3]])
                        if j >= i0:
                            S.op("dve", lambda e: e.tensor_tensor(out=spj[:, c0:c0 + 128], in0=spj[:, c0:c0 + 128], in1=msk_sp[:], op=ALU.mult), reads=[cB], writes=[spB[k % 3]])

                    def S3(k, i0=i0, pr=pr, jof=jof, c0f=c0f, nkb=nkb):
                        j, c0, z = jof(k), c0f(k), ZB[k % 3]
                        spj = Rv(O_SP + (k % 3) * 512, 512)
                        S.op("pe", lambda e: e.matmul(P[2][0:1, c0:512], lhsT=ones_bf[:, 0:1], rhs=spj[:, c0:512], start=(k == 0), stop=(k == nkb - 1), skip_group_check=True),
                             reads=[spB[k % 3], cB], writes=[PB[2]])
                        S.op("pe", lambda e: e.matmul(P[z][:, c0:512], lhsT=ntri[:], rhs=spj[:, c0:512], start=False, stop=False, skip_group_check=True), reads=[spB[k % 3], cB], writes=[PB[z]])
                        if k > 0:
                            S.op("pe", lambda e: e.matmul(P[z][:, c0:512], lhsT=sel[:, 0:128], rhs=csp[:, c0:512], start=False, stop=False, skip_group_check=True), reads=[cspB, cB], writes=[PB[z]])
                        if j >= i0:
                            S.op("pe", lambda e: e.matmul(P[z][:, c0:c0 + 128], lhsT=ident[:], rhs=negmask[:], start=False, stop=True, skip_group_check=True), reads=[cB], writes=[PB[z]])

                    def S4(k, c0f=c0f):
                        c0 = c0f(k)
                        S.op("dve", lambda e: e.tensor_copy(out=csp[0:1, c0:512], in_=P[2][0:1, c0:512]), writes=[PB[2], cspB])
                        S.op("dve", lambda e: e.scalar_tensor_tensor(out=csp[32:33, c0:512], in0=P[2][0:1, c0:512], scalar=1.0, in1=csp[0:1, c0:512], op0=ALU.mult, op1=ALU.subtract),
                             writes=[PB[2], cspB])

                    def S5(k, c0f=c0f):
                        c0, z = c0f(k), ZB[k % 3]
                        wT = Rv(O_WT + (k % 2) * 512, 512)
                        S.op("act", lambda e: e.activation(out=wT[:, c0:512], in_=P[z][:, c0:512], func=AF.Exp), writes=[PB[z], wtB[k % 2]])

                    def S6(k, jof=jof, c0f=c0f, nkb=nkb, po=po):
                        j, c0 = jof(k), c0f(k)
                        wT = Rv(O_WT + (k % 2) * 512, 512)
                        S.op("pe", lambda e: e.matmul(P[po][:, c0:512], lhsT=v2d[:, j * 128:(j + 1) * 128], rhs=wT[:, c0:512], start=(k == 0), stop=(k == nkb - 1), skip_group_check=True),
                             reads=[wtB[k % 2], vB], writes=[PB[po]])

                    for it in range(nkb + 3):
                        if 0 <= it - 1 < nkb:
                            S2b(it - 1)
                        if 0 <= it - 2 < nkb:
                            S3(it - 2)
                            if it - 2 < nkb - 1:
                                S4(it - 2)
                            S5(it - 2)
                        if 0 <= it - 3 < nkb:
                            S6(it - 3)
                        if it < nkb:
                            S1(it)
                            S2a(it)
                    dst = Rv(O_AOT + g * SEQ + i0 * 128, 512)[pr, :]
                    S.op("dve", lambda e, dst=dst, pr=pr, po=po: e.tensor_copy(out=dst, in_=P[po][pr, :]), writes=[PB[po], aoB[c]])

        def da_attention(l, h):
            qm = [qk2d(0), qk2d(1)]
            km = [qk2d(2), qk2d(3)]
            blocks = [(c, j) for c in range(8) for j in range(2 * c + 2)]

            def A(bi):
                c, j = blocks[bi]
                i0 = 2 * c
                c0 = max(0, j - i0) * 128
                pz = bi % 2
                isd = j >= i0
                for m in range(2):
                    S.op("pe", lambda e, m=m: e.matmul(P[pz][:, m * 256 + c0:(m + 1) * 256], lhsT=km[m][0:67, j * 128:(j + 1) * 128],
                                                       rhs=qm[m][0:67, i0 * 128 + c0:i0 * 128 + 256], start=True, stop=(not isd)), reads=[qkB], writes=[PB[pz]])
                    if isd:
                        S.op("pe", lambda e, m=m: e.matmul(P[pz][:, m * 256 + c0:m * 256 + c0 + 128], lhsT=ident[:], rhs=dcorr[:, h * 128:(h + 1) * 128], start=False, stop=True),
                             reads=[cB], writes=[PB[pz]])

            def rng(bi):
                c, j = blocks[bi]
                c0 = max(0, j - 2 * c) * 128
                return [(0, 512)] if c0 == 0 else [(128, 256), (384, 512)]

            def E(bi):
                pz = bi % 2
                eb = Rv(O_EB + pz * 512, 512)
                for (a, b_) in rng(bi):
                    S.op("act", lambda e, a=a, b_=b_: e.activation(out=eb[:, a:b_], in_=P[pz][:, a:b_], func=AF.Exp), writes=[PB[pz], ebB[pz]])

            def V(bi):
                c, j = blocks[bi]
                nkb = 2 * c + 2
                pz = bi % 2
                eb = Rv(O_EB + pz * 512, 512)
                po = 3 + (c % 2)
                pd = 5 if c % 2 == 0 else 2
                rs_ = rng(bi)
                for ri, (a, b_) in enumerate(rs_):
                    lastr = ri == len(rs_) - 1
                    S.op("pe", lambda e, a=a, b_=b_, lastr=lastr: e.matmul(P[po][:, a:b_], lhsT=v2d[:, j * 128:(j + 1) * 128], rhs=eb[:, a:b_], start=(j == 0), stop=(j == nkb - 1 and lastr)),
                         reads=[ebB[pz], vB], writes=[PB[po]])
                for ri, (a, b_) in enumerate(rs_):
                    lastr = ri == len(rs_) - 1
                    S.op("pe", lambda e, a=a, b_=b_, lastr=lastr: e.matmul(P[pd][:, a:b_], lhsT=ones_bf[:], rhs=eb[:, a:b_], start=(j == 0), stop=(j == nkb - 1 and lastr)),
                         reads=[ebB[pz], cB], writes=[PB[pd]])
                if j == nkb - 1:
                    F1(c, po, pd)
                    pend.append([3, c, po])

            pend = []

            def F1(c, po, pd):
                rec, o12, o, rs = w32(0), w32(1), w32(2), w32(3)
                S.op("dve", lambda e: e.reciprocal(out=rec, in_=P[pd][:, :]), writes=[PB[pd], w32B[0]])
                S.op("dve", lambda e: e.tensor_tensor(out=o12, in0=P[po][:, :], in1=rec, op=ALU.mult), reads=[w32B[0]], writes=[PB[po], w32B[1]])
                S.op("dve", lambda e: e.scalar_tensor_tensor(out=o[:, 0:256], in0=o12[:, 256:512], scalar=lamv[:, 8 + l:9 + l], in1=o12[:, 0:256], op0=ALU.mult, op1=ALU.add),
                     reads=[w32B[1], cB], writes=[w32B[2]])
                osq = Rv(O_JUNK, 256)
                S.op("dve", lambda e: e.tensor_tensor(out=osq, in0=o[:, 0:256], in1=o[:, 0:256], op=ALU.mult), reads=[w32B[2]], writes=[junkB])

            def F2(c, po):
                i0 = 2 * c
                rec, o12, o, rs = w32(0), w32(1), w32(2), w32(3)
                osq = Rv(O_JUNK, 256)
                S.op("pe", lambda e: e.matmul(P[po][:, 0:256], lhsT=ones_bf[:], rhs=osq, start=True, stop=True), reads=[junkB, cB], writes=[PB[po]])
                S.op("act", lambda e: e.activation(out=rs[:, 0:256], in_=P[po][:, 0:256], func=AF.Ln, bias=EPS, scale=1.0 / 128), writes=[PB[po], w32B[3]])
                S.op("act", lambda e: e.activation(out=rs[:, 0:256], in_=rs[:, 0:256], func=AF.Exp, scale=-0.5), writes=[w32B[3]])
                S.op("dve", lambda e: e.tensor_tensor(out=o[:, 0:256], in0=o[:, 0:256], in1=rs[:, 0:256], op=ALU.mult), reads=[w32B[3]], writes=[w32B[2]])
                dst = Rv(O_AOT + (4 + h) * SEQ + i0 * 128, 256)
                S.op("act", lambda e: e.activation(out=dst, in_=o[:, 0:256], func=AF.Copy, scale=gsub[:, l:l + 1]), reads=[w32B[2], cB], writes=[aoB[c // 2]])

            def tick():
                for p_ in list(pend):
                    p_[0] -= 1
                    if p_[0] <= 0:
                        pend.remove(p_)
                        F2(p_[1], p_[2])

            nb = len(blocks)
            for bi in range(nb):
                A(bi)
                E(bi)
                if bi > 0:
                    V(bi - 1)
                tick()
            V(nb - 1)
            while pend:
                tick()

        wout3 = hT[:, 0:8192].rearrange("p (k n) -> p k n", n=1024)

        def out_proj(l, b):
            S.dma("pool", lambda e: e.dma_start(out=wout3, in_=w_out_d[l].rearrange("(k p) n -> p k n", p=128)), "wout", writes=hTB)
            make_gp(l, b, 0)
            for tt in range(NT):
                pa, pb_ = (0, 1) if tt % 2 == 0 else (3, 4)
                for hf, pi in enumerate((pa, pb_)):
                    for k in range(8):
                        S.op("pe", lambda e, k=k, hf=hf, pi=pi, tt=tt: e.matmul(P[pi][:, :], lhsT=Rv(O_AOT + k * SEQ + tt * 128, 128), rhs=wout3[:, k, hf * 512:(hf + 1) * 512], start=(k == 0), stop=(k == 7)),
                             reads=[aoB[tt // 4]] + hTB, writes=[PB[pi]])
                postnorm_residual(tt, pa, pb_)

        wdn3 = Rv(O_WDN, 22528).rearrange("p (f n) -> p f n", n=1024)

        def ffn(l, b):
            S.alias(FFN_BUFS, ATT_BUFS)
            S.alias(w32B, w32B)
            prenorm(l, b, 1, O_ACT)
            for f4 in range(0, NFC, 2):
                S.dma("pool", lambda e, f4=f4: e.dma_start(out=wdn3[:, f4:f4 + 2, :], in_=w_down_d[l, f4 * 128:(f4 + 2) * 128, :].rearrange("(f p) n -> p f n", p=128)), "wdn", writes=[wdnB])
            make_gp(l, b, 1)
            S.op("dve", lambda e: e.memset(halo[:], 0.0), writes=[haloB])
            convB = S.buf("conv")
            S.dma("sp", lambda e: e.dma_start(out=convT[:], in_=convT_d[:, l * 176:(l + 1) * 176]), "conv", writes=[convB, cB])
            cw = lambda i, f: convT[:, i * 44 + f:i * 44 + f + 1]
            for qd in range(4):
                for cg in range(11):
                    slot = cg % 2
                    wup3 = Rv(O_WUP + slot * 4096, 4096).rearrange("p (k n) -> p k n", n=512)
                    S.dma("pool", lambda e, cg=cg, wup3=wup3: e.dma_start(out=wup3[:, :, 0:256], in_=w_up_d[l, :, cg * 256:(cg + 1) * 256].rearrange("(k p) n -> p k n", p=128)), f"wup{slot}", writes=[wupB[slot]])
                    S.dma("pool", lambda e, cg=cg, wup3=wup3: e.dma_start(out=wup3[:, :, 256:512], in_=w_up_d[l, :, DFF + cg * 256:DFF + (cg + 1) * 256].rearrange("(k p) n -> p k n", p=128)), f"wup{slot}", writes=[wupB[slot]])
                    for cc in range(2):
                        fc = 2 * cg + cc
                        u3 = W32[:, 0:1028].rearrange("p (g n) -> p g n", n=514)
                        halo3 = halo[:, fc * 4:(fc + 1) * 4].rearrange("p (g t) -> p g t", t=2)
                        for gv in range(2):
                            for k in range(8):
                                S.op("pe", lambda e, k=k, gv=gv, cc=cc, wup3=wup3, qd=qd: e.matmul(P[gv][:, :], lhsT=wup3[:, k, gv * 256 + cc * 128:gv * 256 + (cc + 1) * 128],
                                                                                        rhs=hT[:, k * SEQ + qd * 512:k * SEQ + (qd + 1) * 512], start=(k == 0), stop=(k == 7)),
                                     reads=[wupB[slot], hTB[qd]], writes=[PB[gv]])
                        S.op("dve", lambda e, u3=u3, halo3=halo3: e.tensor_copy(out=u3[:, :, 0:2], in_=halo3), reads=[haloB], writes=[w32B[0], w32B[1]])
                        ys = []
                        for gv in range(2):
                            u = W32[:, gv * 514:(gv + 1) * 514]
                            fidx = gv * NFC + fc
                            y = W32[:, 1028 + gv * 512:1028 + (gv + 1) * 512]
                            S.op("act", lambda e, u=u, gv=gv: e.activation(out=u[:, 2:514], in_=P[gv][:, :], func=AF.Copy), writes=[PB[gv], w32B[gv]])
                            S.op("act", lambda e, y=y, gv=gv, fidx=fidx: e.activation(out=y, in_=P[gv][:, :], func=AF.Identity, scale=cw(2, fidx), bias=cw(3, fidx)),
                                 reads=[cB], writes=[PB[gv], w32B[2 + gv]])
                            ys.append(y)
                        S.op("dve", lambda e, u3=u3, halo3=halo3: e.tensor_copy(out=halo3, in_=u3[:, :, 512:514]), reads=[w32B[0], w32B[1]], writes=[haloB])
                        for gv in range(2):
                            u = W32[:, gv * 514:(gv + 1) * 514]
                            fidx = gv * NFC + fc
                            y = ys[gv]
                            S.op("dve", lambda e, u=u, y=y, fidx=fidx: e.scalar_tensor_tensor(out=y, in0=u[:, 1:513], scalar=cw(1, fidx), in1=y, op0=ALU.mult, op1=ALU.add),
                                 reads=[w32B[gv], cB], writes=[w32B[2 + gv]])
                            S.op("dve", lambda e, u=u, y=y, fidx=fidx: e.scalar_tensor_tensor(out=y, in0=u[:, 0:512], scalar=cw(0, fidx), in1=y, op0=ALU.mult, op1=ALU.add),
                                 reads=[w32B[gv], cB], writes=[w32B[2 + gv]])
                        sg = W32[:, 2052:2052 + 512]
                        S.op("act", lambda e, sg=sg, y=ys[0]: e.activation(out=sg, in_=y, func=AF.Silu), reads=[w32B[2]], writes=[w32B[4]])
                        dst = Rv(O_ACT + fc * 512, 512)
                        S.op("dve", lambda e, sg=sg, y=ys[1], dst=dst: e.tensor_tensor(out=dst, in0=sg, in1=y, op=ALU.mult), reads=[w32B[4], w32B[3]], writes=[actB[fc]])
                for t in range(4):
                    tt = 4 * qd + t
                    pa, pb_ = (3, 4) if t % 2 == 0 else (2, 5)
                    for hf, pi in enumerate((pa, pb_)):
                        for f in range(NFC):
                            S.op("pe", lambda e, f=f, hf=hf, pi=pi, t=t: e.matmul(P[pi][:, :], lhsT=Rv(O_ACT + f * 512 + t * 128, 128), rhs=wdn3[:, f, hf * 512:(hf + 1) * 512], start=(f == 0), stop=(f == NFC - 1)),
                                 reads=[actB[f], wdnB], writes=[PB[pi]])
                    postnorm_residual_ffn(tt, pa, pb_)
            S.alias(ATT_BUFS, FFN_BUFS)
            S.alias(w32B, w32B)

        def postnorm_residual_ffn(tt, pa, pb_):
            junk = W32[:, 1028:1028 + 512]
            for hf, pi in enumerate((pa, pb_)):
                S.op("act", lambda e, hf=hf, pi=pi: e.activation(out=junk, in_=P[pi][:, :], func=AF.Square, accum_out=small[:, 56 + hf:57 + hf]), writes=[PB[pi], w32B[2], smallB])
            S.op("dve", lambda e: e.tensor_tensor(out=small[:, 58:59], in0=small[:, 56:57], in1=small[:, 57:58], op=ALU.add), writes=[smallB])
            S.op("act", lambda e: e.activation(out=small[:, 59:60], in_=small[:, 58:59], func=AF.Ln, bias=EPS, scale=1.0 / D), writes=[smallB])
            S.op("act", lambda e: e.activation(out=small[:, 59:60], in_=small[:, 59:60], func=AF.Exp, scale=-0.5), writes=[smallB])
            for hf, pi in enumerate((pa, pb_)):
                tmp = W32[:, 2052:2052 + 512]
                S.op("dve", lambda e, hf=hf, pi=pi, tmp=tmp: e.scalar_tensor_tensor(out=tmp, in0=P[pi][:, :], scalar=small[:, 59:60], in1=gp[:, hf * 512:(hf + 1) * 512], op0=ALU.mult, op1=ALU.mult),
                     reads=[smallB, gpB], writes=[PB[pi], w32B[4]])
                xs = x_sb[:, tt * 1024 + hf * 512:tt * 1024 + (hf + 1) * 512]
                S.op("dve", lambda e, xs=xs, tmp=tmp: e.tensor_tensor(out=xs, in0=xs, in1=tmp, op=ALU.add), reads=[w32B[4]], writes=[xB[tt]])

        for b in range(nseq):
            for q4 in range(4):
                S.dma("sp", lambda e, b=b, q4=q4: e.dma_start(out=x_sb[:, q4 * 4096:(q4 + 1) * 4096].rearrange("p (t d) -> p t d", d=D),
                                                             in_=x_d[b, q4 * 512:(q4 + 1) * 512, :].rearrange("(t p) d -> p t d", p=128)), f"xin{q4}", writes=xB[4 * q4:4 * q4 + 4])
            for l in range(depth):
                on2 = (lambda st: on(st)) if l == 0 else (lambda st: STAGES1 is None or st in STAGES1)
                if on2("pre"):
                    prenorm(l, b, 0, O_SP)
                for g in range(8):
                    if on2("proj"):
                        project_group(l, g)
                    if g < 4:
                        if on2("sb"):
                            sb_attention(g)
                    else:
                        if on2("da"):
                            da_attention(l, g - 4)
                if on2("out"):
                    out_proj(l, b)
                if on2("ffn"):
                    ffn(l, b)
            for q4 in range(4):
                S.dma("sp", lambda e, b=b, q4=q4: e.dma_start(out=y_d[b, q4 * 512:(q4 + 1) * 512, :].rearrange("(t p) d -> p t d", p=128),
                                                             in_=x_sb[:, q4 * 4096:(q4 + 1) * 4096].rearrange("p (t d) -> p t d", d=D)), f"yout{q4}", reads=xB[4 * q4:4 * q4 + 4])
        S.emit(final_wait_chans=[f"yout{q}" for q in range(4)])
    return nc


def make_in_maps(inputs, ncores=NCORES):
    f = lambda a: np.ascontiguousarray(np.asarray(a, dtype=np.float32))
    x = f(inputs["x"]); c = f(inputs["c"])
    consts = host_consts()
    vec = np.stack([f(inputs[k]) for k in ("attn_pre_g", "attn_post_g", "ffn_pre_g", "ffn_post_g")], 0)
    vecT = np.ascontiguousarray(vec.reshape(4, DEPTH, 8, 128).transpose(3, 0, 1, 2).reshape(128, -1))
    cw = f(inputs["conv_w"]); cb = f(inputs["conv_b"])
    conv = np.concatenate([cw, cb[:, None, :]], 1)
    convT = np.ascontiguousarray(conv.reshape(DEPTH, 4, 44, 128).transpose(3, 0, 1, 2).reshape(128, -1))
    sublnT = np.ascontiguousarray(f(inputs["da_subln_g"]).T)
    lam = np.stack([f(inputs[k]) for k in ("lambda_q1", "lambda_k1", "lambda_q2", "lambda_k2")], 1)
    lamrep = np.ascontiguousarray(np.broadcast_to(lam.reshape(1, -1), (128, DEPTH * 4 * 64)))
    ada_b2 = np.ascontiguousarray(np.broadcast_to(f(inputs["ada_b"])[:, None, :], (DEPTH, 2, 6 * D)))
    shared = {"ada_w": f(inputs["ada_w"]), "ada_b2": ada_b2, "w_in": f(inputs["w_in"]), "w_out": f(inputs["w_out"]),
              "w_up": f(inputs["w_up"]), "w_down": f(inputs["w_down"]), "vecT": vecT, "convT": convT, "sublnT": sublnT, "lamrep": lamrep}
    for k, v in consts.items():
        shared["c_" + k] = np.ascontiguousarray(v)
    maps = []
    for i in range(ncores):
        m = dict(shared)
        m["x"] = np.ascontiguousarray(x[2 * i:2 * i + 2])
        m["cT"] = np.ascontiguousarray(c[2 * i:2 * i + 2].T)
        maps.append(m)
    return maps


_NC = None


def kernel(**inputs):
    global _NC
    if _NC is None:
        _NC = build()
    maps = make_in_maps(inputs)
    res = run_bass_kernel_spmd(_NC, maps, core_ids=list(range(NCORES)))
    out = np.concatenate([np.asarray(r["y"], dtype=np.float32) for r in res.results], axis=0)
    return out
```

```python
import math
from contextlib import ExitStack
import numpy as np
import concourse.bass as bass
import concourse.mybir as mybir
from concourse.bass_utils import run_bass_kernel_spmd

F32 = mybir.dt.float32
BF16 = mybir.dt.bfloat16
AF = mybir.ActivationFunctionType
ALU = mybir.AluOpType

D = 1024
SEQ = 2048
NT = 16
DEPTH = 4
DFF = 2816
NFC = 22
EPS = 1e-6
NCORES = 8
SLOPES = [2.0 ** (-8.0 * (h + 1) / 4) for h in range(4)]

SAME_ENGINE_SYNC = True
GEN = 30000


class Buf:
    __slots__ = ("name", "last_w", "readers")

    def __init__(self, name):
        self.name = name
        self.last_w = None
        self.readers = []


class Op:
    __slots__ = ("eng", "fn", "deps", "needs_inc", "seq", "is_dma", "chan", "cum")

    def __init__(self, eng, fn, is_dma=False, chan=None):
        self.eng = eng
        self.fn = fn
        self.deps = []
        self.needs_inc = False
        self.seq = None
        self.is_dma = is_dma
        self.chan = chan
        self.cum = None


class Sched:
    ENGS = ("pe", "act", "dve", "pool", "sp")

    def __init__(self, nc):
        self.nc = nc
        self.ops = {e: [] for e in self.ENGS}
        self.chan_count = {}

    def buf(self, name="b"):
        return Buf(name)

    def bufs(self, n, name="b"):
        return [Buf(f"{name}{i}") for i in range(n)]

    def alias(self, news, olds):
        for nb in news:
            for ob in olds:
                if ob.last_w is not None:
                    nb.readers.append(ob.last_w)
                nb.readers.extend(ob.readers)

    def _record(self, op, reads, writes):
        deps = []
        for b in reads:
            if b.last_w is not None:
                deps.append((b.last_w, True))
        for b in writes:
            if b.last_w is not None:
                deps.append((b.last_w, True))
            deps.extend((r, False) for r in b.readers)
        seen = set()
        for d, is_w in deps:
            if d is op or id(d) in seen:
                continue
            if d.eng == op.eng and not d.is_dma and not op.is_dma:
                if op.eng == "pe" or not SAME_ENGINE_SYNC or not is_w:
                    continue
            seen.add(id(d))
            op.deps.append(d)
            d.needs_inc = True
        for b in reads:
            b.readers.append(op)
        for b in writes:
            b.last_w = op
            b.readers = []
        self.ops[op.eng].append(op)
        return op

    def op(self, eng, fn, reads=(), writes=()):
        return self._record(Op(eng, fn), reads, writes)

    def dma(self, eng, fn, chan, reads=(), writes=()):
        op = Op(eng, fn, is_dma=True, chan=chan)
        self.chan_count[chan] = self.chan_count.get(chan, 0) + 1
        op.cum = self.chan_count[chan]
        return self._record(op, reads, writes)

    def emit(self, final_wait_chans=()):
        nc = self.nc
        nsem = {}
        for e in self.ENGS:
            c = 0
            for op in self.ops[e]:
                if op.is_dma:
                    continue
                if op.needs_inc:
                    c += 1
                    op.seq = c
            nsem[e] = (c + GEN - 1) // GEN if c else 0
        with ExitStack() as es:
            esems = {e: [es.enter_context(nc.semaphore(f"s_{e}{i}")) for i in range(nsem[e])] for e in self.ENGS}
            csems = {c: es.enter_context(nc.semaphore(f"c_{c}")) for c in self.chan_count}
            block = es.enter_context(nc.Block())

            def run(ename, eng):
                known = {}
                for op in self.ops[ename]:
                    need = {}
                    for d in op.deps:
                        if d.is_dma:
                            key = ("c", d.chan)
                            val = 16 * d.cum
                            sem = csems[d.chan]
                        else:
                            g = (d.seq - 1) // GEN
                            key = (d.eng, g)
                            val = d.seq - g * GEN
                            sem = esems[d.eng][g]
                        if known.get(key, 0) >= val:
                            continue
                        if key not in need or need[key][1] < val:
                            need[key] = (sem, val)
                    for key, (sem, val) in need.items():
                        eng.wait_ge(sem, val)
                        known[key] = val
                    ins = op.fn(eng)
                    if op.is_dma:
                        ins.then_inc(csems[op.chan], 16)
                    elif op.needs_inc:
                        g = (op.seq - 1) // GEN
                        ins.then_inc(esems[ename][g], 1)
                if ename == "sp":
                    for c in final_wait_chans:
                        eng.wait_ge(csems[c], 16 * self.chan_count[c])

            block.tensor(lambda eng: run("pe", eng))
            block.scalar(lambda eng: run("act", eng))
            block.vector(lambda eng: run("dve", eng))
            block.gpsimd(lambda eng: run("pool", eng))
            block.sync(lambda eng: run("sp", eng))


def host_consts():
    j = np.arange(128)[:, None]
    s = np.arange(128)[None, :]
    c = {}
    c["ident"] = np.eye(128, dtype=np.float32)
    c["ntri"] = np.where(j >= s, -1.0, 0.0).astype(np.float32)
    sel = np.zeros((128, 128), np.float32)
    sel[0, :] = -1.0
    sel[32, :] = -1.0
    c["sel"] = sel
    oc = np.zeros((128, 16, 16), np.float32)
    for jb in range(16):
        oc[:, jb, jb] = 1.0
    c["onescol"] = oc.reshape(128, 256)
    c["msk_sp"] = (j < s).astype(np.float32)
    c["negmask"] = np.where(j < s, 0.0, -30000.0).astype(np.float32)
    dc = np.zeros((128, 4, 128), np.float32)
    for h in range(4):
        p = j
        f = s
        val = np.where(f >= p, 0.0, 2.0 * SLOPES[h] * (f - p))
        allowed = (p // 64) <= (f // 64)
        dc[:, h, :] = np.where(allowed, val, -30000.0)
    c["dcorr"] = dc.reshape(128, 512)
    c["ones"] = np.ones((128, 128), np.float32)
    t = np.arange(SEQ)
    aug = np.zeros((3, 4, 2, SEQ), np.float32)
    for h in range(4):
        aug[0, h, 0] = 1.0
        aug[1, h, 0] = 1.0
        aug[2, h, 0] = -SLOPES[h] * 128.0 * (t // 128)
        aug[0, h, 1] = SLOPES[h] * (t % 128)
        aug[1, h, 1] = SLOPES[h] * 128.0 * (t // 128)
        aug[2, h, 1] = 1.0
    c["aug"] = aug.reshape(3, 8 * SEQ)
    c["ident32"] = np.eye(128, dtype=np.float32)
    return c


CONST_SHAPES = {"ident": [128, 128], "ntri": [128, 128], "sel": [128, 128], "onescol": [128, 256], "msk_sp": [128, 128],
                "negmask": [128, 128], "dcorr": [128, 512], "ones": [128, 128], "aug": [3, 8 * SEQ], "ident32": [128, 128]}


STAGES = None
STAGES1 = None


def on(st):
    return STAGES is None or st in STAGES


def build(depth=DEPTH, nseq=2):
    nc = bass.Bass("TRN2", target_bir_lowering=False)
    dr = {}

    def din(name, shape):
        dr[name] = nc.dram_tensor(name, shape, F32, kind="ExternalInput").ap()
        return dr[name]

    x_d = din("x", [2, SEQ, D])
    cT_d = din("cT", [D, 2])
    ada_w_d = din("ada_w", [DEPTH, D, 6 * D])
    ada_b_d = din("ada_b2", [DEPTH, 2, 6 * D])
    w_in_d = din("w_in", [DEPTH, D, 3072])
    w_out_d = din("w_out", [DEPTH, D, D])
    w_up_d = din("w_up", [DEPTH, D, 2 * DFF])
    w_down_d = din("w_down", [DEPTH, DFF, D])
    vecT_d = din("vecT", [128, 4 * DEPTH * 8])
    convT_d = din("convT", [128, DEPTH * 4 * 44])
    sublnT_d = din("sublnT", [128, DEPTH])
    lamrep_d = din("lamrep", [128, DEPTH * 4 * 64])
    cd = {k: din("c_" + k, v) for k, v in CONST_SHAPES.items()}
    y_d = nc.dram_tensor("y", [2, SEQ, D], F32, kind="ExternalOutput").ap()

    es = ExitStack()
    with es:
        def sb(name, shape, dt=F32):
            return es.enter_context(nc.sbuf_tensor("s_" + name, shape, dt))

        def ps(name, shape, dt=F32):
            return es.enter_context(nc.psum_tensor("p_" + name, shape, dt))

        S = Sched(nc)
        x_sb = sb("x_sb", [128, NT * D])
        hT = sb("hT", [128, 8 * SEQ], BF16)
        R = sb("R", [128, 41984], BF16)
        W32 = sb("W32", [128, 3072])
        gp = sb("gp", [128, D])
        ident = sb("ident", [128, 128], BF16)
        ntri = sb("ntri", [128, 128], BF16)
        sel = sb("sel", [128, 128], BF16)
        onescol = sb("onescol", [128, 256], BF16)
        msk_sp = sb("msk_sp", [128, 128], BF16)
        negmask = sb("negmask", [128, 128], BF16)
        dcorr = sb("dcorr", [128, 512], BF16)
        ones_bf = sb("ones_bf", [128, 128], BF16)
        ident32 = sb("ident32", [128, 128])
        ones32 = sb("ones32", [128, 128])
        csp = sb("csp", [128, 1024], BF16)
        modT = sb("modT", [128, DEPTH * 48 * 2])
        vecT = sb("vecT", [128, 4 * DEPTH * 8])
        convT = sb("convT", [128, 4 * 44])
        sublnT = sb("sublnT", [128, DEPTH])
        lamv = sb("lamv", [128, 16])
        gsub = sb("gsub", [128, DEPTH])
        cact = sb("cact", [128, 16])
        small = sb("small", [128, 128])
        halo = sb("halo", [128, 44 * 2])
        diag = sb("diag", [128, 128])

        P = [ps(f"P{i}", [128, 512]) for i in range(6)]
        PT = [ps(f"PT{i}", [128, 1024], BF16) for i in range(2)]
        PB = S.bufs(6, "PB")
        PTB = S.bufs(2, "PTB")

        O_AOT = 0
        O_QK = 16384
        O_V = O_QK + 8192
        O_SP = O_V + 2048
        O_WIN = O_SP + 8192
        O_WT = O_WIN + 3072
        O_EB = O_WT + 1024
        O_JUNK = O_EB + 1024
        O_ACT = 0
        O_WDN = 11264
        O_WUP = O_WDN + 22528

        def Rv(off, n):
            return R[:, off:off + n]

        xB = S.bufs(NT, "x")
        hTB = S.bufs(4, "hT")
        aoB = S.bufs(4, "ao")
        qkB = S.buf("qk")
        vB = S.buf("v")
        spB = S.bufs(16, "sp")
        winB = S.buf("win")
        wtB = S.bufs(2, "wt")
        ebB = S.bufs(2, "eb")
        junkB = S.buf("junk")
        actB = S.bufs(NFC, "act")
        wdnB = S.buf("wdn")
        wupB = S.bufs(2, "wup")
        w32B = S.bufs(6, "w32")
        gpB = S.buf("gp")
        cspB = S.buf("csp")
        cspBB = S.bufs(2, "cspd")
        rsbB = S.buf("rsb")
        smallB = S.buf("small")
        modB = S.buf("mod")
        cB = S.buf("consts")
        haloB = S.buf("halo")
        diagB = S.buf("diag")
        woutB = S.buf("wout")
        xnA = S.buf("xnA")
        xnF = S.buf("xnF")
        ATT_BUFS = aoB + [qkB, vB] + spB + [winB] + wtB + ebB + [junkB, xnA]
        FFN_BUFS = actB + [wdnB] + wupB + [xnF]

        def w32(i, n=512):
            return W32[:, i * 512:i * 512 + n]

        def cload(tile, name, eng="pool"):
            S.dma(eng, lambda e: e.dma_start(out=tile[:], in_=cd[name]), "cst", writes=[cB])

        cload(ident, "ident"); cload(ntri, "ntri"); cload(sel, "sel"); cload(onescol, "onescol")
        cload(msk_sp, "msk_sp"); cload(negmask, "negmask"); cload(dcorr, "dcorr"); cload(ones_bf, "ones")
        cload(ident32, "ident32", "sp"); cload(ones32, "ones", "sp")
        S.dma("sp", lambda e: e.dma_start(out=vecT[:], in_=vecT_d), "cst", writes=[cB])
        S.dma("sp", lambda e: e.dma_start(out=sublnT[:], in_=sublnT_d), "cst", writes=[cB])
        lamrep = W32[:, 0:1024]
        S.dma("sp", lambda e: e.dma_start(out=lamrep, in_=lamrep_d), "cst", writes=[cB, w32B[0], w32B[1]])
        S.dma("sp", lambda e: e.dma_start(out=cact[:].rearrange("p (k b) -> p k b", b=2), in_=cT_d.rearrange("(k p) b -> p k b", p=128)), "cst", writes=[cB])
        S.op("dve", lambda e: e.memset(csp[:], 0.0), writes=[cspB])
        S.op("dve", lambda e: e.memset(halo[:], 0.0), writes=[haloB])
        S.op("act", lambda e: e.activation(out=cact[:], in_=cact[:], func=AF.Silu), reads=[cB], writes=[cB])
        S.op("dve", lambda e: e.tensor_tensor(out=lamrep[:].rearrange("p (l w d) -> p l w d", w=4, d=64)[:, :, 0:4:2, :],
                                                in0=lamrep[:].rearrange("p (l w d) -> p l w d", w=4, d=64)[:, :, 0:4:2, :],
                                                in1=lamrep[:].rearrange("p (l w d) -> p l w d", w=4, d=64)[:, :, 1:4:2, :], op=ALU.mult),
             reads=[cB], writes=[cB, w32B[0], w32B[1]])
        S.op("dve", lambda e: e.tensor_reduce(out=lamv[:, 0:8].rearrange("p (l w) -> p l w", w=2),
                                                in_=lamrep[:].rearrange("p (l w d) -> p l w d", w=4, d=64)[:, :, 0:4:2, :],
                                                axis=mybir.AxisListType.X, op=ALU.add), reads=[cB, w32B[0], w32B[1]], writes=[cB])
        S.op("act", lambda e: e.activation(out=lamv[:, 0:8], in_=lamv[:, 0:8], func=AF.Exp), reads=[cB], writes=[cB])
        for l in range(DEPTH):
            li = 0.8 - 0.6 * math.exp(-0.3 * l)
            S.op("dve", lambda e, l=l, li=li: e.tensor_tensor(out=lamv[:, 8 + l:9 + l], in0=lamv[:, 2 * l + 1:2 * l + 2], in1=lamv[:, 2 * l:2 * l + 1], op=ALU.subtract),
                 reads=[cB], writes=[cB])
            S.op("dve", lambda e, l=l, li=li: e.tensor_scalar(out=lamv[:, 8 + l:9 + l], in0=lamv[:, 8 + l:9 + l], scalar1=-li, scalar2=None, op0=ALU.add),
                 reads=[cB], writes=[cB])
            S.op("dve", lambda e, l=l, li=li: e.tensor_scalar(out=gsub[:, l:l + 1], in0=sublnT[:, l:l + 1], scalar1=(1.0 - li), scalar2=None, op0=ALU.mult),
                 reads=[cB], writes=[cB])

        MR = 8 * 1024
        for l in range(depth):
            S.dma("sp", lambda e, l=l: e.dma_start(out=x_sb[0:2, MR:MR + 6144], in_=ada_b_d[l]), "adab", writes=[xB[8]])
            for ch in range(12):
                slot = ch % 2
                dst = x_sb[:, slot * 4096:(slot + 1) * 4096].rearrange("p (k n) -> p k n", n=512)
                S.dma("sp", lambda e, l=l, ch=ch, dst=dst: e.dma_start(out=dst, in_=ada_w_d[l, :, ch * 512:(ch + 1) * 512].rearrange("(k p) n -> p k n", p=128)),
                      f"adaw{slot}", writes=[xB[slot]])
                for k in range(8):
                    S.op("pe", lambda e, k=k, slot=slot: e.matmul(P[2][0:2, :], lhsT=cact[:, 2 * k:2 * k + 2], rhs=x_sb[:, slot * 4096 + k * 512:slot * 4096 + (k + 1) * 512],
                                                                 start=(k == 0), stop=(k == 7)), reads=[xB[slot], cB], writes=[PB[2]])
                S.op("dve", lambda e, ch=ch: e.tensor_tensor(out=x_sb[0:2, MR + ch * 512:MR + (ch + 1) * 512], in0=P[2][0:2, :],
                                                             in1=x_sb[0:2, MR + ch * 512:MR + (ch + 1) * 512], op=ALU.add),
                     writes=[PB[2], xB[8]])
            for j in range(48):
                S.op("pe", lambda e, j=j: e.matmul(P[5][:, 2 * j:2 * j + 2], lhsT=x_sb[0:2, MR + j * 128:MR + (j + 1) * 128], rhs=ident32[0:2, 0:2], start=True, stop=True),
                     reads=[xB[8], cB], writes=[PB[5]])
            S.op("dve", lambda e, l=l: e.tensor_copy(out=modT[:, l * 96:(l + 1) * 96], in_=P[5][:, 0:96]), writes=[PB[5], modB])

        def modcol(l, j, b):
            o = l * 96 + j * 2 + b
            return modT[:, o:o + 1]

        def modvec(l, grp, b):
            o = l * 96 + grp * 16 + b
            return modT[:, o:o + 15:2]

        def vec(which, l):
            o = (which * DEPTH + l) * 8
            return vecT[:, o:o + 8]

        def prenorm(l, b, which, xn_off):
            sh_g, sc_g = (0, 1) if which == 0 else (3, 4)
            pre = vec(0 if which == 0 else 2, l)
            S.op("dve", lambda e: e.tensor_scalar(out=small[:, 32:40], in0=modvec(l, sc_g, b), scalar1=1.0, scalar2=None, op0=ALU.add), reads=[modB], writes=[smallB])
            S.op("dve", lambda e: e.tensor_tensor(out=small[:, 32:40], in0=small[:, 32:40], in1=pre, op=ALU.mult), reads=[cB], writes=[smallB])
            S.op("dve", lambda e: e.tensor_copy(out=small[:, 40:48], in_=modvec(l, sh_g, b)), reads=[modB], writes=[smallB])
            xnB = xnA if which == 0 else xnF
            for c4 in range(4):
                for t in range(4):
                    tt = 4 * c4 + t
                    xn = Rv(xn_off + t * 1024, 1024)
                    xs = x_sb[:, tt * 1024:(tt + 1) * 1024]
                    S.op("act", lambda e, xn=xn, xs=xs, tt=tt: e.activation(out=xn, in_=xs, func=AF.Square, accum_out=small[:, tt:tt + 1]), reads=[xB[tt]], writes=[xnB, smallB])
                    S.op("act", lambda e, tt=tt: e.activation(out=small[:, 16 + tt:17 + tt], in_=small[:, tt:tt + 1], func=AF.Ln, bias=EPS, scale=1.0 / D), writes=[smallB])
                    S.op("act", lambda e, tt=tt: e.activation(out=small[:, 16 + tt:17 + tt], in_=small[:, 16 + tt:17 + tt], func=AF.Exp, scale=-0.5), writes=[smallB])
                    S.op("dve", lambda e, xn=xn, xs=xs, tt=tt: e.tensor_scalar(out=xn, in0=xs, scalar1=small[:, 16 + tt:17 + tt], scalar2=None, op0=ALU.mult),
                         reads=[xB[tt], smallB], writes=[xnB])
                for k in range(8):
                    pt = k % 2
                    for t in range(4):
                        S.op("pe", lambda e, k=k, t=t, pt=pt: e.transpose(PT[pt][:, t * 128:(t + 1) * 128], Rv(xn_off + t * 1024 + k * 128, 128), ident[:]),
                             reads=[xnB, cB], writes=[PTB[pt]])
                    dst = hT[:, k * SEQ + c4 * 512:k * SEQ + (c4 + 1) * 512]
                    if k % 2 == 0:
                        S.op("act", lambda e, k=k, pt=pt, dst=dst: e.activation(out=dst, in_=PT[pt][:, 0:512], func=AF.Identity, scale=small[:, 32 + k:33 + k], bias=small[:, 40 + k:41 + k]),
                             reads=[smallB], writes=[PTB[pt], hTB[c4]])
                    else:
                        S.op("dve", lambda e, k=k, pt=pt, dst=dst: e.tensor_scalar(out=dst, in0=PT[pt][:, 0:512], scalar1=small[:, 32 + k:33 + k], scalar2=small[:, 40 + k:41 + k], op0=ALU.mult, op1=ALU.add),
                             reads=[smallB], writes=[PTB[pt], hTB[c4]])

        def make_gp(l, b, which):
            g_g = 2 if which == 0 else 5
            post = vec(1 if which == 0 else 3, l)
            S.op("dve", lambda e: e.tensor_tensor(out=small[:, 48:56], in0=modvec(l, g_g, b), in1=post, op=ALU.mult), reads=[modB, cB], writes=[smallB])
            for k in range(8):
                pb = 2 if k < 4 else 5
                S.op("dve", lambda e, k=k: e.tensor_scalar(out=diag[:], in0=ident32[:], scalar1=small[:, 48 + k:49 + k], scalar2=None, op0=ALU.mult), reads=[smallB, cB], writes=[diagB])
                S.op("pe", lambda e, k=k, pb=pb: e.matmul(P[pb][:, (k % 4) * 128:(k % 4 + 1) * 128], lhsT=ones32[:], rhs=diag[:], start=True, stop=True), reads=[diagB, cB], writes=[PB[pb]])
                if k % 4 == 3:
                    S.op("act", lambda e, k=k, pb=pb: e.activation(out=gp[:, (k // 4) * 512:(k // 4 + 1) * 512], in_=P[pb][:, :], func=AF.Copy), writes=[PB[pb], gpB])

        def postnorm_residual(tt, pa, pb_):
            junk = Rv(O_JUNK, 512)
            for hf, pi in enumerate((pa, pb_)):
                S.op("act", lambda e, hf=hf, pi=pi: e.activation(out=junk, in_=P[pi][:, :], func=AF.Square, accum_out=small[:, 56 + hf:57 + hf]), writes=[PB[pi], junkB, smallB])
            S.op("dve", lambda e: e.tensor_tensor(out=small[:, 58:59], in0=small[:, 56:57], in1=small[:, 57:58], op=ALU.add), writes=[smallB])
            S.op("act", lambda e: e.activation(out=small[:, 59:60], in_=small[:, 58:59], func=AF.Ln, bias=EPS, scale=1.0 / D), writes=[smallB])
            S.op("act", lambda e: e.activation(out=small[:, 59:60], in_=small[:, 59:60], func=AF.Exp, scale=-0.5), writes=[smallB])
            for hf, pi in enumerate((pa, pb_)):
                tmp = w32(4 + hf)
                S.op("dve", lambda e, hf=hf, pi=pi, tmp=tmp: e.scalar_tensor_tensor(out=tmp, in0=P[pi][:, :], scalar=small[:, 59:60], in1=gp[:, hf * 512:(hf + 1) * 512], op0=ALU.mult, op1=ALU.mult),
                     reads=[smallB, gpB], writes=[PB[pi], w32B[4 + hf]])
                xs = x_sb[:, tt * 1024 + hf * 512:tt * 1024 + (hf + 1) * 512]
                S.op("dve", lambda e, xs=xs, tmp=tmp: e.tensor_tensor(out=xs, in0=xs, in1=tmp, op=ALU.add), reads=[w32B[4 + hf]], writes=[xB[tt]])

        qk2d = lambda i: Rv(O_QK + i * 2048, 2048)
        v2d = Rv(O_V, 2048)
        win3 = Rv(O_WIN, 3072).rearrange("p (k n) -> p k n", n=384)

        def project_group(l, g):
            is_da = g >= 4
            if not is_da:
                cq, ck, cv = 128 * g, 512 + 128 * g, 1024 + 128 * g
            else:
                h = g - 4
                cq, ck, cv = 1536 + 128 * h, 2048 + 128 * h, 2560 + 128 * h
            for i, c0 in enumerate((cq, ck, cv)):
                S.dma("pool", lambda e, i=i, c0=c0: e.dma_start(out=win3[:, :, i * 128:(i + 1) * 128], in_=w_in_d[l, :, c0:c0 + 128].rearrange("(k p) n -> p k n", p=128)),
                      "win", writes=[winB])
            if is_da:
                h = g - 4
                for i in range(4):
                    qk = 0 if i < 2 else 1
                    S.dma("pool", lambda e, i=i, qk=qk, h=h: e.dma_start(out=qk2d(i)[64:67, :], in_=cd["aug"][:, (h * 2 + qk) * SEQ:(h * 2 + qk + 1) * SEQ]), "aug", writes=[qkB])
            for c4 in range(4):
                for i in range(2):
                    pz = (2 * c4 + i) % 2
                    for k in range(8):
                        S.op("pe", lambda e, i=i, k=k, pz=pz, c4=c4: e.matmul(P[pz][:, :], lhsT=win3[:, k, i * 128:(i + 1) * 128], rhs=hT[:, k * SEQ + c4 * 512:k * SEQ + (c4 + 1) * 512],
                                                                       start=(k == 0), stop=(k == 7)), reads=[winB, hTB[c4]], writes=[PB[pz]])
                    sc = 0.125 if i == 0 else 1.0
                    cols = slice(c4 * 512, (c4 + 1) * 512)
                    if not is_da:
                        dst = qk2d(i)[:, cols]
                        S.op("act", lambda e, dst=dst, pz=pz, sc=sc: e.activation(out=dst, in_=P[pz][:, :], func=AF.Copy, scale=sc), writes=[PB[pz], qkB])
                    else:
                        d0 = qk2d(2 * i)[0:64, cols]
                        d1 = qk2d(2 * i + 1)[0:64, cols]
                        S.op("act", lambda e, d0=d0, pz=pz, sc=sc: e.activation(out=d0, in_=P[pz][0:64, :], func=AF.Copy, scale=sc), writes=[PB[pz], qkB])
                        S.op("dve", lambda e, d1=d1, pz=pz, sc=sc: e.tensor_scalar(out=d1, in0=P[pz][64:128, :], scalar1=sc, scalar2=None, op0=ALU.mult), writes=[PB[pz], qkB])
                pv = 2 if c4 % 2 == 0 else 5
                for t in range(4):
                    tt = 4 * c4 + t
                    for k in range(8):
                        S.op("pe", lambda e, t=t, tt=tt, k=k, pv=pv: e.matmul(P[pv][:, t * 128:(t + 1) * 128], lhsT=hT[:, k * SEQ + tt * 128:k * SEQ + (tt + 1) * 128], rhs=win3[:, k, 256:384],
                                                                    start=(k == 0), stop=(k == 7)), reads=[winB, hTB[c4]], writes=[PB[pv]])
                S.op("dve", lambda e, c4=c4, pv=pv: e.tensor_copy(out=v2d[:, c4 * 512:(c4 + 1) * 512], in_=P[pv][:, :]), writes=[PB[pv], vB])

        def sb_attention(g):
            qT, kT = qk2d(0), qk2d(1)
            ZB = [0, 1, 3]
            cnt = [0]
            for hh in range(2):
                pr = slice(hh * 64, hh * 64 + 64)
                for c in range(4):
                    i0 = 4 * c
                    nkb = i0 + 4
                    po = 5
                    cnt[0] += 1
                    S.op("dve", lambda e: e.memset(csp[0:64, :], 0.0), writes=cspBB)
                    jof = lambda k, nkb=nkb: nkb - 1 - k
                    c0f = lambda k, i0=i0, nkb=nkb: max(0, (nkb - 1 - k) - i0) * 128

                    def S1(k, i0=i0, pr=pr, jof=jof, c0f=c0f):
                        j, c0, z = jof(k), c0f(k), ZB[k % 3]
                        S.op("pe", lambda e: e.matmul(P[z][:, c0:512], lhsT=kT[pr, j * 128:(j + 1) * 128], rhs=qT[pr, i0 * 128 + c0:i0 * 128 + 512], start=True, stop=True),
                             reads=[qkB], writes=[PB[z]])

                    def S2a(k, c0f=c0f):
                        c0, z = c0f(k), ZB[k % 3]
                        e32 = w32(k % 2)
                        S.op("act", lambda e: e.activation(out=e32[:, c0:512], in_=P[z][:, c0:512], func=AF.Exp), writes=[PB[z], w32B[k % 2]])

                    def S2b(k, i0=i0, jof=jof, c0f=c0f):
                        j, c0 = jof(k), c0f(k)
                        e32 = w32(k % 2)
                        spj = Rv(O_SP + (k % 3) * 512, 512)
                        S.op("act", lambda e: e.activation(out=spj[:, c0:512], in_=e32[:, c0:512], func=AF.Ln, bias=1.0, scale=1.0), reads=[w32B[k % 2]], writes=[spB[k % 3]])
                        if j >= i0:
                            S.op("dve", lambda e: e.tensor_tensor(out=spj[:, c0:c0 + 128], in0=spj[:, c0:c0 + 128], in1=msk_sp[:], op=ALU.mult), reads=[cB], writes=[spB[k % 3]])

                    RBK = [2, 4]

                    def S3(k, i0=i0, pr=pr, jof=jof, c0f=c0f, nkb=nkb):
                        j, c0, z = jof(k), c0f(k), ZB[k % 3]
                        spj = Rv(O_SP + (k % 3) * 512, 512)
                        rb = RBK[k % 2]
                        if k < nkb - 1:
                            S.op("pe", lambda e: e.matmul(P[rb][0:1, c0:512], lhsT=ones_bf[:, 0:1], rhs=spj[:, c0:512], start=True, stop=True),
                                 reads=[spB[k % 3], cB], writes=[PB[rb]])
                        S.op("pe", lambda e: e.matmul(P[z][:, c0:512], lhsT=ntri[:], rhs=spj[:, c0:512], start=False, stop=False, skip_group_check=True), reads=[spB[k % 3], cB], writes=[PB[z]])
                        if k > 0:
                            cs = csp[:, ((k - 1) % 2) * 512:((k - 1) % 2) * 512 + 512]
                            S.op("pe", lambda e: e.matmul(P[z][:, c0:512], lhsT=sel[:, 0:128], rhs=cs[:, c0:512], start=False, stop=False, skip_group_check=True), reads=[cspBB[(k - 1) % 2], cB], writes=[PB[z]])
                        if j >= i0:
                            S.op("pe", lambda e: e.matmul(P[z][:, c0:c0 + 128], lhsT=ident[:], rhs=negmask[:], start=False, stop=True, skip_group_check=True), reads=[cB], writes=[PB[z]])

                    def S4(k, c0f=c0f):
                        c0 = c0f(k)
                        rb = RBK[k % 2]
                        cs = csp[:, (k % 2) * 512:(k % 2) * 512 + 512]
                        rsb = w32(3)
                        if k == 0:
                            S.op("dve", lambda e: e.tensor_copy(out=rsb[0:1, c0:512], in_=P[rb][0:1, c0:512]), writes=[PB[rb], rsbB, w32B[3]])
                        else:
                            c0p = c0f(k - 1)
                            if c0 < c0p:
                                S.op("dve", lambda e: e.tensor_copy(out=rsb[0:1, c0:c0p], in_=P[rb][0:1, c0:c0p]), writes=[PB[rb], rsbB, w32B[3]])
                            S.op("dve", lambda e: e.tensor_tensor(out=rsb[0:1, c0p:512], in0=rsb[0:1, c0p:512], in1=P[rb][0:1, c0p:512], op=ALU.add), writes=[PB[rb], rsbB, w32B[3]])
                        S.op("dve", lambda e: e.tensor_copy(out=cs[0:1, c0:512], in_=rsb[0:1, c0:512]), reads=[rsbB], writes=[cspBB[k % 2]])
                        S.op("dve", lambda e: e.scalar_tensor_tensor(out=cs[32:33, c0:512], in0=rsb[0:1, c0:512], scalar=1.0, in1=cs[0:1, c0:512], op0=ALU.mult, op1=ALU.subtract),
                             reads=[rsbB], writes=[cspBB[k % 2]])

                    def S5(k, c0f=c0f):
                        c0, z = c0f(k), ZB[k % 3]
                        wT = Rv(O_WT + (k % 2) * 512, 512)
                        S.op("act", lambda e: e.activation(out=wT[:, c0:512], in_=P[z][:, c0:512], func=AF.Exp), writes=[PB[z], wtB[k % 2]])

                    def S6(k, jof=jof, c0f=c0f, nkb=nkb, po=po):
                        j, c0 = jof(k), c0f(k)
                        wT = Rv(O_WT + (k % 2) * 512, 512)
                        S.op("pe", lambda e: e.matmul(P[po][:, c0:512], lhsT=v2d[:, j * 128:(j + 1) * 128], rhs=wT[:, c0:512], start=(k == 0), stop=(k == nkb - 1), skip_group_check=True),
                             reads=[wtB[k % 2], vB], writes=[PB[po]])

                    for it in range(nkb + 3):
                        if 0 <= it - 1 < nkb:
                            S2b(it - 1)
                        if 0 <= it - 2 < nkb:
                            S3(it - 2)
                            if it - 2 < nkb - 1:
                                S4(it - 2)
                            S5(it - 2)
                        if 0 <= it - 3 < nkb:
                            S6(it - 3)
                        if it < nkb:
                            S1(it)
                            S2a(it)
                    dst = Rv(O_AOT + g * SEQ + i0 * 128, 512)[pr, :]
                    S.op("dve", lambda e, dst=dst, pr=pr, po=po: e.tensor_copy(out=dst, in_=P[po][pr, :]), writes=[PB[po], aoB[c]])

        def da_attention(l, h):
            qm = [qk2d(0), qk2d(1)]
            km = [qk2d(2), qk2d(3)]
            blocks = [(c, j) for c in range(8) for j in range(2 * c + 2)]

            def A(bi):
                c, j = blocks[bi]
                i0 = 2 * c
                c0 = max(0, j - i0) * 128
                pz = bi % 2
                isd = j >= i0
                for m in range(2):
                    S.op("pe", lambda e, m=m: e.matmul(P[pz][:, m * 256 + c0:(m + 1) * 256], lhsT=km[m][0:67, j * 128:(j + 1) * 128],
                                                       rhs=qm[m][0:67, i0 * 128 + c0:i0 * 128 + 256], start=True, stop=(not isd)), reads=[qkB], writes=[PB[pz]])
                    if isd:
                        S.op("pe", lambda e, m=m: e.matmul(P[pz][:, m * 256 + c0:m * 256 + c0 + 128], lhsT=ident[:], rhs=dcorr[:, h * 128:(h + 1) * 128], start=False, stop=True),
                             reads=[cB], writes=[PB[pz]])

            def rng(bi):
                c, j = blocks[bi]
                c0 = max(0, j - 2 * c) * 128
                return [(0, 512)] if c0 == 0 else [(128, 256), (384, 512)]

            def E(bi):
                pz = bi % 2
                eb = Rv(O_EB + pz * 512, 512)
                for (a, b_) in rng(bi):
                    S.op("act", lambda e, a=a, b_=b_: e.activation(out=eb[:, a:b_], in_=P[pz][:, a:b_], func=AF.Exp), writes=[PB[pz], ebB[pz]])

            def V(bi):
                c, j = blocks[bi]
                nkb = 2 * c + 2
                pz = bi % 2
                eb = Rv(O_EB + pz * 512, 512)
                po = 3 + (c % 2)
                pd = 5 if c % 2 == 0 else 2
                rs_ = rng(bi)
                for ri, (a, b_) in enumerate(rs_):
                    lastr = ri == len(rs_) - 1
                    S.op("pe", lambda e, a=a, b_=b_, lastr=lastr: e.matmul(P[po][:, a:b_], lhsT=v2d[:, j * 128:(j + 1) * 128], rhs=eb[:, a:b_], start=(j == 0), stop=(j == nkb - 1 and lastr)),
                         reads=[ebB[pz], vB], writes=[PB[po]])
                for ri, (a, b_) in enumerate(rs_):
                    lastr = ri == len(rs_) - 1
                    S.op("pe", lambda e, a=a, b_=b_, lastr=lastr: e.matmul(P[pd][:, a:b_], lhsT=ones_bf[:], rhs=eb[:, a:b_], start=(j == 0), stop=(j == nkb - 1 and lastr)),
                         reads=[ebB[pz], cB], writes=[PB[pd]])
                if j == nkb - 1:
                    F1(c, po, pd)
                    pend.append([3, c, po])

            pend = []

            def F1(c, po, pd):
                rec, o12, o, rs = w32(0), w32(1), w32(2), w32(3)
                S.op("dve", lambda e: e.reciprocal(out=rec, in_=P[pd][:, :]), writes=[PB[pd], w32B[0]])
                S.op("dve", lambda e: e.tensor_tensor(out=o12, in0=P[po][:, :], in1=rec, op=ALU.mult), reads=[w32B[0]], writes=[PB[po], w32B[1]])
                S.op("dve", lambda e: e.scalar_tensor_tensor(out=o[:, 0:256], in0=o12[:, 256:512], scalar=lamv[:, 8 + l:9 + l], in1=o12[:, 0:256], op0=ALU.mult, op1=ALU.add),
                     reads=[w32B[1], cB], writes=[w32B[2]])
                osq = Rv(O_JUNK, 256)
                S.op("dve", lambda e: e.tensor_tensor(out=osq, in0=o[:, 0:256], in1=o[:, 0:256], op=ALU.mult), reads=[w32B[2]], writes=[junkB])

            def F2(c, po):
                i0 = 2 * c
                rec, o12, o, rs = w32(0), w32(1), w32(2), w32(3)
                osq = Rv(O_JUNK, 256)
                S.op("pe", lambda e: e.matmul(P[po][:, 0:256], lhsT=ones_bf[:], rhs=osq, start=True, stop=True), reads=[junkB, cB], writes=[PB[po]])
                S.op("act", lambda e: e.activation(out=rs[:, 0:256], in_=P[po][:, 0:256], func=AF.Ln, bias=EPS, scale=1.0 / 128), writes=[PB[po], w32B[3]])
                S.op("act", lambda e: e.activation(out=rs[:, 0:256], in_=rs[:, 0:256], func=AF.Exp, scale=-0.5), writes=[w32B[3]])
                S.op("dve", lambda e: e.tensor_tensor(out=o[:, 0:256], in0=o[:, 0:256], in1=rs[:, 0:256], op=ALU.mult), reads=[w32B[3]], writes=[w32B[2]])
                dst = Rv(O_AOT + (4 + h) * SEQ + i0 * 128, 256)
                S.op("act", lambda e: e.activation(out=dst, in_=o[:, 0:256], func=AF.Copy, scale=gsub[:, l:l + 1]), reads=[w32B[2], cB], writes=[aoB[c // 2]])

            def tick():
                for p_ in list(pend):
                    p_[0] -= 1
                    if p_[0] <= 0:
                        pend.remove(p_)
                        F2(p_[1], p_[2])

            nb = len(blocks)
            for bi in range(nb):
                A(bi)
                E(bi)
                if bi > 0:
                    V(bi - 1)
                tick()
            V(nb - 1)
            while pend:
                tick()

        wout3 = hT[:, 0:8192].rearrange("p (k n) -> p k n", n=1024)

        def out_proj(l, b):
            S.dma("pool", lambda e: e.dma_start(out=wout3, in_=w_out_d[l].rearrange("(k p) n -> p k n", p=128)), "wout", writes=hTB)
            make_gp(l, b, 0)
            for tt in range(NT):
                pa, pb_ = (0, 1) if tt % 2 == 0 else (3, 4)
                for hf, pi in enumerate((pa, pb_)):
                    for k in range(8):
                        S.op("pe", lambda e, k=k, hf=hf, pi=pi, tt=tt: e.matmul(P[pi][:, :], lhsT=Rv(O_AOT + k * SEQ + tt * 128, 128), rhs=wout3[:, k, hf * 512:(hf + 1) * 512], start=(k == 0), stop=(k == 7)),
                             reads=[aoB[tt // 4]] + hTB, writes=[PB[pi]])
                postnorm_residual(tt, pa, pb_)

        wdn3 = Rv(O_WDN, 22528).rearrange("p (f n) -> p f n", n=1024)

        def ffn(l, b):
            S.alias(FFN_BUFS, ATT_BUFS)
            S.alias(w32B, w32B)
            prenorm(l, b, 1, O_ACT)
            for f4 in range(0, NFC, 2):
                S.dma("pool", lambda e, f4=f4: e.dma_start(out=wdn3[:, f4:f4 + 2, :], in_=w_down_d[l, f4 * 128:(f4 + 2) * 128, :].rearrange("(f p) n -> p f n", p=128)), "wdn", writes=[wdnB])
            make_gp(l, b, 1)
            S.op("dve", lambda e: e.memset(halo[:], 0.0), writes=[haloB])
            convB = S.buf("conv")
            S.dma("sp", lambda e: e.dma_start(out=convT[:], in_=convT_d[:, l * 176:(l + 1) * 176]), "conv", writes=[convB, cB])
            cw = lambda i, f: convT[:, i * 44 + f:i * 44 + f + 1]
            for qd in range(4):
                for cg in range(11):
                    slot = cg % 2
                    wup3 = Rv(O_WUP + slot * 4096, 4096).rearrange("p (k n) -> p k n", n=512)
                    S.dma("pool", lambda e, cg=cg, wup3=wup3: e.dma_start(out=wup3[:, :, 0:256], in_=w_up_d[l, :, cg * 256:(cg + 1) * 256].rearrange("(k p) n -> p k n", p=128)), f"wup{slot}", writes=[wupB[slot]])
                    S.dma("pool", lambda e, cg=cg, wup3=wup3: e.dma_start(out=wup3[:, :, 256:512], in_=w_up_d[l, :, DFF + cg * 256:DFF + (cg + 1) * 256].rearrange("(k p) n -> p k n", p=128)), f"wup{slot}", writes=[wupB[slot]])
                    for cc in range(2):
                        fc = 2 * cg + cc
                        u3 = W32[:, 0:1028].rearrange("p (g n) -> p g n", n=514)
                        halo3 = halo[:, fc * 4:(fc + 1) * 4].rearrange("p (g t) -> p g t", t=2)
                        for gv in range(2):
                            for k in range(8):
                                S.op("pe", lambda e, k=k, gv=gv, cc=cc, wup3=wup3, qd=qd: e.matmul(P[gv][:, :], lhsT=wup3[:, k, gv * 256 + cc * 128:gv * 256 + (cc + 1) * 128],
                                                                                        rhs=hT[:, k * SEQ + qd * 512:k * SEQ + (qd + 1) * 512], start=(k == 0), stop=(k == 7)),
                                     reads=[wupB[slot], hTB[qd]], writes=[PB[gv]])
                        S.op("dve", lambda e, u3=u3, halo3=halo3: e.tensor_copy(out=u3[:, :, 0:2], in_=halo3), reads=[haloB], writes=[w32B[0], w32B[1]])
                        ys = []
                        for gv in range(2):
                            u = W32[:, gv * 514:(gv + 1) * 514]
                            fidx = gv * NFC + fc
                            y = W32[:, 1028 + gv * 512:1028 + (gv + 1) * 512]
                            S.op("act", lambda e, u=u, gv=gv: e.activation(out=u[:, 2:514], in_=P[gv][:, :], func=AF.Copy), writes=[PB[gv], w32B[gv]])
                            S.op("act", lambda e, y=y, gv=gv, fidx=fidx: e.activation(out=y, in_=P[gv][:, :], func=AF.Identity, scale=cw(2, fidx), bias=cw(3, fidx)),
                                 reads=[cB], writes=[PB[gv], w32B[2 + gv]])
                            ys.append(y)
                        S.op("dve", lambda e, u3=u3, halo3=halo3: e.tensor_copy(out=halo3, in_=u3[:, :, 512:514]), reads=[w32B[0], w32B[1]], writes=[haloB])
                        for gv in range(2):
                            u = W32[:, gv * 514:(gv + 1) * 514]
                            fidx = gv * NFC + fc
                            y = ys[gv]
                            S.op("dve", lambda e, u=u, y=y, fidx=fidx: e.scalar_tensor_tensor(out=y, in0=u[:, 1:513], scalar=cw(1, fidx), in1=y, op0=ALU.mult, op1=ALU.add),
                                 reads=[w32B[gv], cB], writes=[w32B[2 + gv]])
                            S.op("dve", lambda e, u=u, y=y, fidx=fidx: e.scalar_tensor_tensor(out=y, in0=u[:, 0:512], scalar=cw(0, fidx), in1=y, op0=ALU.mult, op1=ALU.add),
                                 reads=[w32B[gv], cB], writes=[w32B[2 + gv]])
                        sg = W32[:, 2052:2052 + 512]
                        S.op("act", lambda e, sg=sg, y=ys[0]: e.activation(out=sg, in_=y, func=AF.Silu), reads=[w32B[2]], writes=[w32B[4]])
                        dst = Rv(O_ACT + fc * 512, 512)
                        S.op("dve", lambda e, sg=sg, y=ys[1], dst=dst: e.tensor_tensor(out=dst, in0=sg, in1=y, op=ALU.mult), reads=[w32B[4], w32B[3]], writes=[actB[fc]])
                for t in range(4):
                    tt = 4 * qd + t
                    pa, pb_ = (3, 4) if t % 2 == 0 else (2, 5)
                    for hf, pi in enumerate((pa, pb_)):
                        for f in range(NFC):
                            S.op("pe", lambda e, f=f, hf=hf, pi=pi, t=t: e.matmul(P[pi][:, :], lhsT=Rv(O_ACT + f * 512 + t * 128, 128), rhs=wdn3[:, f, hf * 512:(hf + 1) * 512], start=(f == 0), stop=(f == NFC - 1)),
                                 reads=[actB[f], wdnB], writes=[PB[pi]])
                    postnorm_residual_ffn(tt, pa, pb_)
            S.alias(ATT_BUFS, FFN_BUFS)
            S.alias(w32B, w32B)

        def postnorm_residual_ffn(tt, pa, pb_):
            junk = W32[:, 1028:1028 + 512]
            for hf, pi in enumerate((pa, pb_)):
                S.op("act", lambda e, hf=hf, pi=pi: e.activation(out=junk, in_=P[pi][:, :], func=AF.Square, accum_out=small[:, 56 + hf:57 + hf]), writes=[PB[pi], w32B[2], smallB])
            S.op("dve", lambda e: e.tensor_tensor(out=small[:, 58:59], in0=small[:, 56:57], in1=small[:, 57:58], op=ALU.add), writes=[smallB])
            S.op("act", lambda e: e.activation(out=small[:, 59:60], in_=small[:, 58:59], func=AF.Ln, bias=EPS, scale=1.0 / D), writes=[smallB])
            S.op("act", lambda e: e.activation(out=small[:, 59:60], in_=small[:, 59:60], func=AF.Exp, scale=-0.5), writes=[smallB])
            for hf, pi in enumerate((pa, pb_)):
                tmp = W32[:, 2052:2052 + 512]
                S.op("dve", lambda e, hf=hf, pi=pi, tmp=tmp: e.scalar_tensor_tensor(out=tmp, in0=P[pi][:, :], scalar=small[:, 59:60], in1=gp[:, hf * 512:(hf + 1) * 512], op0=ALU.mult, op1=ALU.mult),
                     reads=[smallB, gpB], writes=[PB[pi], w32B[4]])
                xs = x_sb[:, tt * 1024 + hf * 512:tt * 1024 + (hf + 1) * 512]
                S.op("dve", lambda e, xs=xs, tmp=tmp: e.tensor_tensor(out=xs, in0=xs, in1=tmp, op=ALU.add), reads=[w32B[4]], writes=[xB[tt]])

        for b in range(nseq):
            for q4 in range(4):
                S.dma("sp", lambda e, b=b, q4=q4: e.dma_start(out=x_sb[:, q4 * 4096:(q4 + 1) * 4096].rearrange("p (t d) -> p t d", d=D),
                                                             in_=x_d[b, q4 * 512:(q4 + 1) * 512, :].rearrange("(t p) d -> p t d", p=128)), f"xin{q4}", writes=xB[4 * q4:4 * q4 + 4])
            for l in range(depth):
                on2 = (lambda st: on(st)) if l == 0 else (lambda st: STAGES1 is None or st in STAGES1)
                if on2("pre"):
                    prenorm(l, b, 0, O_SP)
                for g in range(8):
                    if on2("proj"):
                        project_group(l, g)
                    if g < 4:
                        if on2("sb"):
                            sb_attention(g)
                    else:
                        if on2("da"):
                            da_attention(l, g - 4)
                if on2("out"):
                    out_proj(l, b)
                if on2("ffn"):
                    ffn(l, b)
            for q4 in range(4):
                S.dma("sp", lambda e, b=b, q4=q4: e.dma_start(out=y_d[b, q4 * 512:(q4 + 1) * 512, :].rearrange("(t p) d -> p t d", p=128),
                                                             in_=x_sb[:, q4 * 4096:(q4 + 1) * 4096].rearrange("p (t d) -> p t d", d=D)), f"yout{q4}", reads=xB[4 * q4:4 * q4 + 4])
        S.emit(final_wait_chans=[f"yout{q}" for q in range(4)])
    return nc


def make_in_maps(inputs, ncores=NCORES):
    f = lambda a: np.ascontiguousarray(np.asarray(a, dtype=np.float32))
    x = f(inputs["x"]); c = f(inputs["c"])
    consts = host_consts()
    vec = np.stack([f(inputs[k]) for k in ("attn_pre_g", "attn_post_g", "ffn_pre_g", "ffn_post_g")], 0)
    vecT = np.ascontiguousarray(vec.reshape(4, DEPTH, 8, 128).transpose(3, 0, 1, 2).reshape(128, -1))
    cw = f(inputs["conv_w"]); cb = f(inputs["conv_b"])
    conv = np.concatenate([cw, cb[:, None, :]], 1)
    convT = np.ascontiguousarray(conv.reshape(DEPTH, 4, 44, 128).transpose(3, 0, 1, 2).reshape(128, -1))
    sublnT = np.ascontiguousarray(f(inputs["da_subln_g"]).T)
    lam = np.stack([f(inputs[k]) for k in ("lambda_q1", "lambda_k1", "lambda_q2", "lambda_k2")], 1)
    lamrep = np.ascontiguousarray(np.broadcast_to(lam.reshape(1, -1), (128, DEPTH * 4 * 64)))
    ada_b2 = np.ascontiguousarray(np.broadcast_to(f(inputs["ada_b"])[:, None, :], (DEPTH, 2, 6 * D)))
    shared = {"ada_w": f(inputs["ada_w"]), "ada_b2": ada_b2, "w_in": f(inputs["w_in"]), "w_out": f(inputs["w_out"]),
              "w_up": f(inputs["w_up"]), "w_down": f(inputs["w_down"]), "vecT": vecT, "convT": convT, "sublnT": sublnT, "lamrep": lamrep}
    for k, v in consts.items():
        shared["c_" + k] = np.ascontiguousarray(v)
    maps = []
    for i in range(ncores):
        m = dict(shared)
        m["x"] = np.ascontiguousarray(x[2 * i:2 * i + 2])
        m["cT"] = np.ascontiguousarray(c[2 * i:2 * i + 2].T)
        maps.append(m)
    return maps


_NC = None


def kernel(**inputs):
    global _NC
    if _NC is None:
        _NC = build()
    maps = make_in_maps(inputs)
    res = run_bass_kernel_spmd(_NC, maps, core_ids=list(range(NCORES)))
    out = np.concatenate([np.asarray(r["y"], dtype=np.float32) for r in res.results], axis=0)
    return out
```
